# Optimizing a Trainium2 kernel written in Bass

```python
import jax, jax.numpy as jnp
from jax import lax
import numpy as np

D_MODEL = 1024
BATCH = 16
SEQ = 2048
DEPTH = 2

N_MIXERS = 2
N_MLA_LAYERS = (DEPTH + 1) // 2
N_HGRN_LAYERS = DEPTH // 2

MLA_HEADS = D_MODEL // 128
MLA_Q_LORA = 3 * D_MODEL // 8
MLA_KV_LORA = D_MODEL // 4
MLA_NOPE = 128
MLA_ROPE = 64
MLA_V = 128
ROPE_THETA = 10000.0
Q_BLOCK = 128

HGRN_EXPAND = 128
HGRN_HEADS = D_MODEL // HGRN_EXPAND
HGRN_F_DIM = HGRN_HEADS * HGRN_EXPAND
HGRN_I_HEAD = D_MODEL // HGRN_HEADS
CHUNK = 64

D_FF = ((8 * D_MODEL // 3 + 127) // 128) * 128
CONV_WIDTH = 3

EPS = 1e-6

kernel_name = "hybrid_mla_hgrn2_convffn_encoder"


def rms_norm(x, gain):
    xf = x.astype(jnp.float32)
    y = xf * lax.rsqrt(jnp.mean(xf * xf, axis=-1, keepdims=True) + EPS)
    return (y * gain.astype(jnp.float32)).astype(x.dtype)


def rope_tables(positions):
    inv_freq = 1.0 / (ROPE_THETA ** (jnp.arange(0, MLA_ROPE, 2, dtype=jnp.float32) / MLA_ROPE))
    ang = positions.astype(jnp.float32)[..., None] * inv_freq
    return jnp.cos(ang), jnp.sin(ang)


def apply_rope(x, cos, sin):
    x1, x2 = jnp.split(x, 2, axis=-1)
    return jnp.concatenate([x1 * cos - x2 * sin, x2 * cos + x1 * sin], axis=-1).astype(x.dtype)


def mla_mixer(h, positions, w_in, q_norm, w_q_up, kv_norm, w_kv_up, w_out):
    B, S, _ = h.shape
    proj = h @ w_in
    c_q, c_kv, k_rope = jnp.split(proj, [MLA_Q_LORA, MLA_Q_LORA + MLA_KV_LORA], axis=-1)
    q = (rms_norm(c_q, q_norm) @ w_q_up).reshape(B, S, MLA_HEADS, MLA_NOPE + MLA_ROPE)
    q_nope, q_rope = q[..., :MLA_NOPE], q[..., MLA_NOPE:]
    kv = (rms_norm(c_kv, kv_norm) @ w_kv_up).reshape(B, S, MLA_HEADS, MLA_NOPE + MLA_V)
    k_nope, v = kv[..., :MLA_NOPE], kv[..., MLA_NOPE:]
    cos, sin = rope_tables(positions)
    q_rope = apply_rope(q_rope, cos[:, :, None, :], sin[:, :, None, :])
    k_rope = apply_rope(k_rope, cos, sin)
    scale = (MLA_NOPE + MLA_ROPE) ** -0.5
    n_blk = S // Q_BLOCK

    def to_blocks(t):
        return jnp.moveaxis(t.reshape(B, n_blk, Q_BLOCK, *t.shape[2:]), 1, 0)

    def attend(blk):
        qn, qr = blk
        s = (jnp.einsum('bqhd,bkhd->bhqk', qn, k_nope)
             + jnp.einsum('bqhr,bkr->bhqk', qr, k_rope))
        p = jax.nn.softmax(s.astype(jnp.float32) * scale, axis=-1).astype(v.dtype)
        return jnp.einsum('bhqk,bkhd->bqhd', p, v)

    o = lax.map(attend, (to_blocks(q_nope), to_blocks(q_rope)))
    o = jnp.moveaxis(o, 0, 1).reshape(B, S, MLA_HEADS * MLA_V)
    return o @ w_out


def chunk_scan(q, k, v, log_f):
    B, H, S, dk = q.shape
    dv = v.shape[-1]
    n = S // CHUNK

    def chunks(t):
        return jnp.moveaxis(t.astype(jnp.float32).reshape(B, H, n, CHUNK, t.shape[-1]), 2, 0)

    lower = jnp.tril(jnp.ones((CHUNK, CHUNK), dtype=bool))[..., None]

    def step(state, xs):
        qc, kc, vc, gc = xs
        b = jnp.cumsum(gc, axis=2)
        o_inter = jnp.einsum('bhtk,bhkv->bhtv', qc * jnp.exp(b), state)
        diff = b[:, :, :, None, :] - b[:, :, None, :, :]
        decay = jnp.exp(jnp.where(lower, diff, -jnp.inf))
        scores = jnp.einsum('bhtk,bhsk,bhtsk->bhts', qc, kc, decay)
        o_intra = jnp.einsum('bhts,bhsv->bhtv', scores, vc)
        b_last = b[:, :, -1, :]
        state = (jnp.exp(b_last)[..., None] * state
                 + jnp.einsum('bhsk,bhsv->bhkv', kc * jnp.exp(b_last[:, :, None, :] - b), vc))
        return state, o_inter + o_intra

    s0 = jnp.zeros((B, H, dk, dv), jnp.float32)
    _, o = lax.scan(step, s0, (chunks(q), chunks(k), chunks(v), chunks(log_f)))
    return jnp.moveaxis(o, 0, 2).reshape(B, H, S, dv)


def hgrn2_mixer(h, layer_idx, w_in, lb_logits, out_norm, w_out):
    B, S, _ = h.shape
    proj = h @ w_in
    q, f_fw, f_bw, i, g = jnp.split(
        proj, [HGRN_F_DIM, 2 * HGRN_F_DIM, 3 * HGRN_F_DIM, 3 * HGRN_F_DIM + D_MODEL], axis=-1)
    probs = jax.nn.softmax(lb_logits.astype(jnp.float32), axis=1)
    lb = (jnp.cumsum(probs, axis=1) - probs[:, :1])[:, layer_idx]

    def heads(t):
        return t.reshape(B, S, HGRN_HEADS, -1).transpose(0, 2, 1, 3)

    q = heads(jax.nn.silu(q)) * HGRN_EXPAND ** -0.5
    i = heads(i)

    def gates(f_raw, lb_d):
        xf = f_raw.astype(jnp.float32)
        log_f = jnp.logaddexp(jnp.log(lb_d), jnp.log1p(-lb_d) + jax.nn.log_sigmoid(xf))
        k = (1.0 - lb_d) * jax.nn.sigmoid(-xf)
        return heads(log_f), heads(k)

    log_f_fw, k_fw = gates(f_fw, lb[0])
    log_f_bw, k_bw = gates(f_bw, lb[1])
    o_fw = chunk_scan(q, k_fw, i, log_f_fw)
    flip = lambda t: jnp.flip(t, axis=2)
    o_bw = flip(chunk_scan(flip(q), flip(k_bw), flip(i), flip(log_f_bw)))
    o = (o_fw + o_bw).transpose(0, 2, 1, 3)
    o = rms_norm(o, out_norm.reshape(HGRN_HEADS, HGRN_I_HEAD))
    o = o.reshape(B, S, D_MODEL).astype(h.dtype) * jax.nn.silu(g)
    return o @ w_out


def conv_ffn(h, w_in, conv_w, conv_b, w_out):
    gate, val = jnp.split(h @ w_in, 2, axis=-1)
    pad = (CONV_WIDTH - 1) // 2
    gate = lax.conv_general_dilated(
        gate, conv_w[:, None, :].astype(gate.dtype), window_strides=(1,),
        padding=[(pad, pad)], dimension_numbers=('NWC', 'WIO', 'NWC'),
        feature_group_count=D_FF) + conv_b
    return (jax.nn.gelu(gate, approximate=True) * val) @ w_out


def _dense(key, shape, fan_in):
    return jax.random.normal(key, shape, jnp.float32) * fan_in ** -0.5


def _gain(key, shape):
    return 1.0 + 0.02 * jax.random.normal(key, shape, jnp.float32)


def setup_inputs(seed: int = 0) -> dict:
    key = jax.random.key(seed)
    ks = jax.random.split(key, 21)
    mla_in = MLA_Q_LORA + MLA_KV_LORA + MLA_ROPE
    return {
        "x": jax.random.normal(ks[0], (BATCH, SEQ, D_MODEL), jnp.float32),
        "positions": (jnp.arange(SEQ, dtype=jnp.int32)[None, :]
                      + jax.random.randint(ks[1], (BATCH, 1), 0, 1024, dtype=jnp.int32)),
        "pre_mix_norm": _gain(ks[2], (DEPTH, D_MODEL)),
        "post_mix_norm": _gain(ks[3], (DEPTH, D_MODEL)),
        "pre_ffn_norm": _gain(ks[4], (DEPTH, D_MODEL)),
        "post_ffn_norm": _gain(ks[5], (DEPTH, D_MODEL)),
        "mla_w_in": _dense(ks[6], (N_MLA_LAYERS, D_MODEL, mla_in), D_MODEL),
        "mla_q_norm": _gain(ks[7], (N_MLA_LAYERS, MLA_Q_LORA)),
        "mla_w_q_up": _dense(ks[8], (N_MLA_LAYERS, MLA_Q_LORA, MLA_HEADS * (MLA_NOPE + MLA_ROPE)), MLA_Q_LORA),
        "mla_kv_norm": _gain(ks[9], (N_MLA_LAYERS, MLA_KV_LORA)),
        "mla_w_kv_up": _dense(ks[10], (N_MLA_LAYERS, MLA_KV_LORA, MLA_HEADS * (MLA_NOPE + MLA_V)), MLA_KV_LORA),
        "mla_w_out": _dense(ks[11], (N_MLA_LAYERS, MLA_HEADS * MLA_V, D_MODEL), MLA_HEADS * MLA_V),
        "hgrn_w_in": _dense(ks[12], (N_HGRN_LAYERS, D_MODEL, 3 * HGRN_F_DIM + 2 * D_MODEL), D_MODEL),
        "hgrn_lb_logits": jax.random.normal(ks[13], (2, DEPTH, HGRN_F_DIM), jnp.float32),
        "hgrn_out_norm": _gain(ks[14], (N_HGRN_LAYERS, D_MODEL)),
        "hgrn_w_out": _dense(ks[15], (N_HGRN_LAYERS, D_MODEL, D_MODEL), D_MODEL),
        "ffn_w_in": _dense(ks[16], (DEPTH, D_MODEL, 2 * D_FF), D_MODEL),
        "ffn_conv_w": _dense(ks[17], (DEPTH, CONV_WIDTH, D_FF), CONV_WIDTH),
        "ffn_conv_b": 0.02 * jax.random.normal(ks[18], (DEPTH, D_FF), jnp.float32),
        "ffn_w_out": _dense(ks[19], (DEPTH, D_FF, D_MODEL), D_FF),
    }


def reference(x, positions, pre_mix_norm, post_mix_norm, pre_ffn_norm, post_ffn_norm,
              mla_w_in, mla_q_norm, mla_w_q_up, mla_kv_norm, mla_w_kv_up, mla_w_out,
              hgrn_w_in, hgrn_lb_logits, hgrn_out_norm, hgrn_w_out,
              ffn_w_in, ffn_conv_w, ffn_conv_b, ffn_w_out):
    for l in range(DEPTH):
        hn = rms_norm(x, pre_mix_norm[l])
        j = l // N_MIXERS
        if l % N_MIXERS == 0:
            m = mla_mixer(hn, positions, mla_w_in[j], mla_q_norm[j], mla_w_q_up[j],
                          mla_kv_norm[j], mla_w_kv_up[j], mla_w_out[j])
        else:
            m = hgrn2_mixer(hn, l, hgrn_w_in[j], hgrn_lb_logits, hgrn_out_norm[j], hgrn_w_out[j])
        x = x + rms_norm(m, post_mix_norm[l])
        hn = rms_norm(x, pre_ffn_norm[l])
        f = conv_ffn(hn, ffn_w_in[l], ffn_conv_w[l], ffn_conv_b[l], ffn_w_out[l])
        x = x + rms_norm(f, post_ffn_norm[l])
    return x
```

```python
import contextlib
import numpy as np
import concourse.bass as bass
import concourse.mybir as mybir
from concourse.bass_utils import run_bass_kernel_spmd

F32 = mybir.dt.float32
BF16 = mybir.dt.bfloat16
I32 = mybir.dt.int32
U8 = mybir.dt.uint8
AF = mybir.ActivationFunctionType
ALU = mybir.AluOpType

COMPUTE = ("pe", "act", "dve", "pool")
ALLENG = COMPUTE + ("sp",)
NDMA_SEMS = 24
CELL = 512
ARENA_BYTES = 206 * 1024
DEBUG_WHERE = False


def _dsize(dt):
    return mybir.dt.size(dt)


class Sched:
    def __init__(self, nc):
        self.nc = nc
        self.ops = []
        self.last_writer = {}
        self.readers = {}
        self.n_dma = 0
        self._cache = {}
        self.names = {}

    def cells(self, a):
        if isinstance(a, str):
            return (a,)
        t = a.tensor
        key = (t.name, a.offset, a.ap, str(a.dtype))
        r = self._cache.get(key)
        if r is not None:
            return r
        nm = t.name
        if nm.startswith("arena"):
            sp = "S"
        elif nm.startswith("psum"):
            sp = "P"
        else:
            r = ()
            self._cache[key] = r
            return r
        ds = _dsize(a.dtype)
        ap = a.ap
        pstride, pcount = ap[0]
        if pstride > 0:
            p0 = a.offset // pstride
            f0 = a.offset % pstride
        else:
            p0 = 0
            f0 = a.offset
        free = ap[1:]
        starts = [f0]
        for (st, cnt) in free[:-1]:
            starts = [s + i * st for s in starts for i in range(cnt)]
        if free:
            lst, lcnt = free[-1]
            span = (lcnt - 1) * abs(lst) + 1
        else:
            span = 1
        pg0 = p0 // 32
        pg1 = (p0 + pcount - 1) // 32
        cs = set()
        cell = CELL if sp == "S" else 2048
        for s in starts:
            b0 = (s * ds) // cell
            b1 = ((s + span) * ds - 1) // cell
            for b in range(b0, b1 + 1):
                for pg in range(pg0, pg1 + 1):
                    cs.add((sp, pg, b))
        r = tuple(cs)
        self._cache[key] = r
        return r

    def op(self, eng, fn, reads=(), writes=(), dma=False):
        idx = len(self.ops)
        deps = {}
        rc = set()
        wc = set()
        for a in reads:
            rc.update(self.cells(a))
        for a in writes:
            wc.update(self.cells(a))
        for k in rc:
            w = self.last_writer.get(k)
            if w is not None:
                deps[w] = True
        for k in wc:
            w = self.last_writer.get(k)
            if w is not None and w not in deps:
                deps[w] = False
            rl = self.readers.get(k)
            if rl:
                for r in rl:
                    if r not in deps:
                        deps[r] = False
        for k in rc:
            if k in wc:
                continue
            rl = self.readers.get(k)
            if rl is None:
                self.readers[k] = [idx]
            else:
                rl.append(idx)
        for k in wc:
            self.last_writer[k] = idx
            self.readers[k] = None
        rec = dict(eng=eng, fn=fn, deps=deps, dma=dma, signal=False, dma_id=None)
        if DEBUG_WHERE:
            import sys as _sys
            f = _sys._getframe(1)
            w = []
            while f is not None and len(w) < 5:
                w.append("%s:%d" % (f.f_code.co_name, f.f_lineno))
                f = f.f_back
            rec["where"] = w
        if dma:
            rec["dma_id"] = self.n_dma
            self.n_dma += 1
        self.ops.append(rec)
        return idx

    def emit(self, final_wait_eng="sp"):
        nc = self.nc
        ops = self.ops
        dma_ops = [i for i, o in enumerate(ops) if o["dma"]]
        for i, o in enumerate(ops):
            nd = {}
            for d, raw in o["deps"].items():
                p = ops[d]
                if (not p["dma"]) and (not o["dma"]) and p["eng"] == o["eng"] == "pe":
                    continue
                nd[d] = raw
            if o["dma"] and o["dma_id"] >= NDMA_SEMS:
                nd[dma_ops[o["dma_id"] - NDMA_SEMS]] = True
            o["deps"] = nd
            for d in nd:
                ops[d]["signal"] = True
        ms = {e: 0 for e in ALLENG}
        for o in ops:
            if o["dma"]:
                o["sig"] = (("dma", o["dma_id"] % NDMA_SEMS), 16 * (o["dma_id"] // NDMA_SEMS + 1))
            elif o["signal"]:
                ms[o["eng"]] += 1
                o["sig"] = (("eng", o["eng"]), ms[o["eng"]])
        with contextlib.ExitStack() as stack:
            sems = {}
            for e in ALLENG:
                sems[("eng", e)] = stack.enter_context(nc.semaphore("s_" + e))
            for j in range(min(NDMA_SEMS, max(self.n_dma, 1))):
                sems[("dma", j)] = stack.enter_context(nc.semaphore("s_dma%d" % j))
            block = stack.enter_context(nc.Block())
            final_dma = {}
            for o in ops:
                if o["dma"]:
                    k, v = o["sig"]
                    final_dma[k] = max(final_dma.get(k, 0), v)

            def make(engname):
                def body(eng):
                    waited = {}
                    for o in ops:
                        if o["eng"] != engname:
                            continue
                        need = {}
                        for d in o["deps"]:
                            k, v = ops[d]["sig"]
                            if v > need.get(k, 0):
                                need[k] = v
                        for k, v in need.items():
                            if waited.get(k, 0) >= v:
                                continue
                            eng.wait_ge(sems[k], v)
                            waited[k] = v
                        ins = o["fn"](eng)
                        if DEBUG_WHERE:
                            try:
                                self.names[str(ins.ins.name)] = o["where"]
                            except Exception:
                                pass
                        if o["dma"]:
                            k, v = o["sig"]
                            ins.then_inc(sems[k], 16)
                        elif o["signal"]:
                            ins.then_inc(sems[("eng", engname)], 1)
                    if engname == final_wait_eng:
                        for k, v in final_dma.items():
                            if waited.get(k, 0) < v:
                                eng.wait_ge(sems[k], v)
                return body

            used = set(o["eng"] for o in ops) | {final_wait_eng}
            if "pe" in used:
                block.tensor(make("pe"))
            if "act" in used:
                block.scalar(make("act"))
            if "dve" in used:
                block.vector(make("dve"))
            if "pool" in used:
                block.gpsimd(make("pool"))
            if "sp" in used:
                block.sync(make("sp"))


class KB:
    def __init__(self, nc, stack):
        self.nc = nc
        self.S = Sched(nc)
        self.arena = stack.enter_context(nc.sbuf_tensor("arena", [128, ARENA_BYTES], U8))
        self.psum = stack.enter_context(nc.psum_tensor("psum", [128, 4096], F32))
        self.top = 0

    def sb(self, shape, dt, align=CELL):
        n = int(np.prod(shape[1:])) * _dsize(dt)
        off = (self.top + align - 1) // align * align
        assert off + n <= ARENA_BYTES, "SBUF arena overflow: need %d" % (off + n)
        self.top = off + n
        v = self.arena[0:shape[0], off:off + n].bitcast(dt)
        if len(shape) == 3:
            v = v.rearrange("p (a b) -> p a b", a=shape[1])
        elif len(shape) == 4:
            v = v.rearrange("p (a b c) -> p a b c", a=shape[1], b=shape[2])
        return v

    def sb_at(self, off, shape, dt):
        save = self.top
        self.top = off
        v = self.sb(shape, dt)
        end = self.top
        self.top = save
        return v, end

    def mark(self):
        return self.top

    def release(self, m):
        self.top = m

    def ps(self, bank, width=512, dt=F32, parts=128, col0=0):
        v = self.psum[0:parts, bank * 512 + col0: bank * 512 + col0 + width]
        if dt != F32:
            v = v.bitcast(dt)
        return v

    def dma(self, out, in_, eng="sp", extra_reads=(), extra_writes=()):
        return self.S.op(eng, lambda e: e.dma_start(out=out, in_=in_),
                         reads=[in_] + list(extra_reads), writes=[out] + list(extra_writes), dma=True)

    def mm(self, out, lhsT, rhs, start=True, stop=True):
        return self.S.op("pe", lambda e: e.matmul(out, lhsT, rhs, start=start, stop=stop),
                         reads=[lhsT, rhs], writes=[out])

    def transpose(self, out, in_, ident):
        return self.S.op("pe", lambda e: e.transpose(out, in_, ident), reads=[in_, ident], writes=[out])

    def act(self, out, in_, func, scale=1.0, bias=None, eng="act"):
        reads = [in_]
        if not isinstance(scale, (int, float)):
            reads.append(scale)
        kw = {}
        if bias is not None:
            kw["bias"] = bias
            if not isinstance(bias, (int, float)):
                reads.append(bias)
        return self.S.op("act", lambda e: e.activation(out=out, in_=in_, func=func, scale=scale, **kw),
                         reads=reads, writes=[out])

    def tt(self, out, in0, in1, op, eng="dve"):
        return self.S.op(eng, lambda e: e.tensor_tensor(out=out, in0=in0, in1=in1, op=op),
                         reads=[in0, in1], writes=[out])

    def ts(self, out, in0, s1, s2, op0, op1=None, eng="dve"):
        reads = [in0]
        for s in (s1, s2):
            if s is not None and not isinstance(s, (int, float)):
                reads.append(s)
        if op1 is None:
            f = lambda e: e.tensor_scalar(out=out, in0=in0, scalar1=s1, scalar2=None, op0=op0)
        else:
            f = lambda e: e.tensor_scalar(out=out, in0=in0, scalar1=s1, scalar2=s2, op0=op0, op1=op1)
        return self.S.op(eng, f, reads=reads, writes=[out])

    def stt(self, out, in0, scalar, in1, op0, op1, eng="dve"):
        reads = [in0, in1]
        if not isinstance(scalar, (int, float)):
            reads.append(scalar)
        return self.S.op(eng, lambda e: e.scalar_tensor_tensor(out=out, in0=in0, scalar=scalar, in1=in1,
                                                               op0=op0, op1=op1),
                         reads=reads, writes=[out])

    def copy(self, out, in_, eng="dve"):
        if eng == "act":
            return self.act(out, in_, AF.Copy)
        return self.S.op(eng, lambda e: e.tensor_copy(out=out, in_=in_), reads=[in_], writes=[out])

    def recip(self, out, in_):
        return self.S.op("dve", lambda e: e.reciprocal(out=out, in_=in_), reads=[in_], writes=[out])

    def memset(self, out, val, eng="pool"):
        return self.S.op(eng, lambda e: e.memset(out, val), writes=[out])

    def scan(self, out, d0, d1, init, op0, op1):
        reads = [d0, d1]
        if not isinstance(init, (int, float)):
            reads.append(init)
        return self.S.op("dve", lambda e: e.tensor_tensor_scan(out=out, data0=d0, data1=d1, initial=init,
                                                               op0=op0, op1=op1),
                         reads=reads, writes=[out])

    def copy_pred(self, out, mask, data):
        return self.S.op("dve", lambda e: e.copy_predicated(out=out, mask=mask, data=data),
                         reads=[mask, data, out], writes=[out])

    def affine_select(self, out, in_, pattern, cmp, fill, base, cm):
        return self.S.op("pool", lambda e: e.affine_select(out=out, in_=in_, pattern=pattern, compare_op=cmp,
                                                           fill=fill, base=base, channel_multiplier=cm),
                         reads=[in_], writes=[out])


P = 128
D = 1024
KC = 8
S = 2048
W = 512
NSEQ = 2
NH = 8
DFF = 2816
NFF = 22
QL, KVL = 384, 256
EPSV = 1e-6
TWO_PI = 6.283185307179586
MAGIC = 12582912.0
C1 = 6.28125
C2 = TWO_PI - C1
PI_LO = 3.1415925

G_OFF = 0
QN_OFF = 64
KVN_OFF = 67
ONG_OFF = 69
LBL_OFF = 77
CW_OFF = 109
CB_OFF = 241
INVF_OFF = 285
NCST = 286


class Prog:
    def __init__(self, nseq=NSEQ, stages=("mla", "ffn0", "hgrn", "ffn1"), dbg=None):
        self.nseq = nseq
        self.stages = stages
        self.dbg = dbg
        nc = bass.Bass("TRN2", target_bir_lowering=False)
        self.nc = nc
        dt = nc.dram_tensor
        self.x = dt("x", [nseq, S, D], F32, kind="ExternalInput").ap()
        self.pos = dt("pos", [nseq, S], I32, kind="ExternalInput").ap()
        self.cst = dt("cst", [P, NCST], F32, kind="ExternalInput").ap()
        self.mla_w_in = dt("mla_w_in", [D, 704], F32, kind="ExternalInput").ap()
        self.mla_w_q_up = dt("mla_w_q_up", [QL, 1536], F32, kind="ExternalInput").ap()
        self.mla_w_kv_up = dt("mla_w_kv_up", [KVL, 2048], F32, kind="ExternalInput").ap()
        self.mla_w_out = dt("mla_w_out", [D, D], F32, kind="ExternalInput").ap()
        self.hgrn_w_in = dt("hgrn_w_in", [D, 5120], F32, kind="ExternalInput").ap()
        self.hgrn_w_out = dt("hgrn_w_out", [D, D], F32, kind="ExternalInput").ap()
        self.ffn_w_in = dt("ffn_w_in", [2, D, 2 * DFF], F32, kind="ExternalInput").ap()
        self.ffn_w_out = dt("ffn_w_out", [2, DFF, D], F32, kind="ExternalInput").ap()
        self.out = dt("out", [nseq, S, D], F32, kind="ExternalOutput").ap()
        with contextlib.ExitStack() as st:
            self.K = KB(nc, st)
            self.setup()
            for s in range(nseq):
                self.sequence(s)
            self.K.S.emit()

    def nb(self):
        b = self._bank
        self._bank = (self._bank + 1) % len(self._banks)
        return self._banks[b]

    def set_banks(self, banks):
        self._banks = list(banks)
        self._bank = 0

    def g(self, col):
        return self.CST[:, col:col + 1]

    def setup(self):
        K = self.K
        self.set_banks(range(8))
        self.CST = K.sb([P, NCST], F32)
        K.dma(self.CST, self.cst)
        self.IDF = K.sb([P, P], F32)
        K.memset(self.IDF, 0.0)
        K.affine_select(self.IDF, self.IDF, [[-1, P]], ALU.not_equal, 1.0, 0, 1)
        self.IDB = K.sb([P, P], BF16)
        K.copy(self.IDB, self.IDF)
        self.ONESB = K.sb([P, P], BF16)
        K.memset(self.ONESB, 1.0)
        self.EPS = K.sb([P, 1], F32)
        K.memset(self.EPS, EPSV)
        self.ONE = K.sb([P, 1], F32)
        K.memset(self.ONE, 1.0)
        self.MASK = []
        for dr in range(2):
            mf = K.sb([P, P], F32)
            K.memset(mf, 1.0)
            if dr == 0:
                K.affine_select(mf, mf, [[1, P]], ALU.is_ge, 0.0, 0, -1)
                K.memset(mf[0:64, 64:128], 0.0)
            else:
                K.affine_select(mf, mf, [[-1, P]], ALU.is_ge, 0.0, 0, 1)
                K.memset(mf[64:128, 0:64], 0.0)
            self.MASK.append(mf)
        self.AM = [K.sb([P, P], BF16) for _ in range(4)]
        for a in self.AM:
            K.memset(a, 0.0)
        self.LB = K.sb([P, 16], F32)
        self.OML = K.sb([P, 16], F32)
        self.NOML = K.sb([P, 16], F32)
        lbl = self.CST[:, LBL_OFF:LBL_OFF + 32].rearrange("p (d l h) -> p d l h", d=2, l=2)
        dv = K.sb([P, 2, 8], F32)
        K.tt(dv, lbl[:, :, 1, :], lbl[:, :, 0, :], ALU.subtract)
        K.act(self.LB.rearrange("p (d h) -> p d h", d=2), dv, AF.Sigmoid)
        K.ts(self.OML, self.LB, -1.0, 1.0, ALU.mult, ALU.add)
        K.ts(self.NOML, self.LB, -1.0, None, ALU.add)
        self.XT = K.sb([P, KC, S], F32)
        self.BIGA_OFF = (K.top + CELL - 1) // CELL * CELL
        self.BIGA = K.sb([P, KC, S], BF16)

    def rstd(self, src3, n, width, nfeat, SQ, SD, RS, bank):
        K = self.K
        K.act(SQ[:, 0:n, 0:width], src3, AF.Square)
        ps = K.ps(bank, width)
        for c in range(n):
            K.mm(ps, self.ONESB, SQ[:, c, 0:width], start=(c == 0), stop=(c == n - 1))
        K.act(SD[:, 0:width], ps, AF.Ln, scale=1.0 / nfeat, bias=self.EPS[:, 0:1])
        K.act(RS[:, 0:width], SD[:, 0:width], AF.Exp, scale=-0.5)

    def norm_tmps(self):
        K = self.K
        return K.sb([P, KC, W], BF16), K.sb([P, W], F32), K.sb([P, W], F32)

    def prologue(self, t0, t1, gcol, dst, tm):
        K = self.K
        SQ, SD, RS = tm
        b = t0
        while b < t1:
            e = min(b + W, t1)
            w = e - b
            self.rstd(self.XT[:, :, b:e], KC, w, D, SQ, SD, RS, self.nb())
            for c in range(KC):
                K.stt(dst[:, c, b - t0:e - t0], self.XT[:, c, b:e], self.g(gcol + c), RS[:, 0:w],
                      ALU.mult, ALU.mult)
            b = e

    def post_norm_residual(self, M3, gcol, t0, tm, Tb):
        K = self.K
        SQ, SD, RS = tm
        self.rstd(M3, KC, W, D, SQ, SD, RS, self.nb())
        for dc in range(KC):
            T = Tb[dc % 2]
            K.stt(T, M3[:, dc, :], self.g(gcol + dc), RS, ALU.mult, ALU.mult)
            K.tt(self.XT[:, dc, t0:t0 + W], self.XT[:, dc, t0:t0 + W], T, ALU.add, eng="pool")

    def epilogue_full(self, Y, wdram, gcol):
        K = self.K
        m = K.mark()
        WO = K.sb([P, KC, D], BF16)
        K.dma(WO, wdram.rearrange("(k p) n -> p k n", p=P), eng="pool")
        M = K.sb([P, KC, W], F32)
        tm = self.norm_tmps()
        Tb = [K.sb([P, W], F32) for _ in range(2)]
        for tb in range(S // W):
            cols = slice(tb * W, (tb + 1) * W)
            for dc in range(KC):
                ps = K.ps(self.nb())
                for k in range(KC):
                    K.mm(ps, WO[:, k, dc * P:(dc + 1) * P], Y[:, k, cols], start=(k == 0), stop=(k == KC - 1))
                K.copy(M[:, dc, :], ps, eng="act")
            self.post_norm_residual(M, gcol, tb * W, tm, Tb)
        K.release(m)

    def sequence(self, s):
        self.set_banks(range(8))
        self.load_x(s)
        if "mla" in self.stages:
            self.mla(s)
        if "ffn0" in self.stages:
            self.ffn(s, 0)
        if "hgrn" in self.stages:
            self.hgrn(s)
        if "ffn1" in self.stages:
            self.ffn(s, 1)
        self.store_x(s)

    def load_x(self, s):
        K = self.K
        m = K.mark()
        XIN = [K.sb([P, D], F32) for _ in range(2)]
        self.set_banks(range(8))
        for t in range(S // P):
            xin = XIN[t % 2]
            K.dma(xin, self.x[s, t * P:(t + 1) * P, :], eng="sp")
            for half in range(2):
                bank = self.nb()
                for cc in range(4):
                    c = half * 4 + cc
                    K.transpose(K.ps(bank, P, col0=cc * P), xin[:, c * P:(c + 1) * P], self.IDF)
                K.copy(self.XT[:, half * 4:(half + 1) * 4, t * P:(t + 1) * P],
                       K.ps(bank).rearrange("p (a b) -> p a b", a=4), eng=("act" if half else "dve"))
        K.release(m)

    def store_x(self, s):
        K = self.K
        m = K.mark()
        XO = [K.sb([P, D], F32) for _ in range(2)]
        self.set_banks(range(8))
        for t in range(S // P):
            xo = XO[t % 2]
            for half in range(2):
                bank = self.nb()
                for cc in range(4):
                    c = half * 4 + cc
                    K.transpose(K.ps(bank, P, col0=cc * P), self.XT[:, c, t * P:(t + 1) * P], self.IDF)
                K.copy(xo[:, half * W:(half + 1) * W], K.ps(bank), eng=("act" if half else "dve"))
            K.dma(self.out[s, t * P:(t + 1) * P, :], xo, eng="sp")
        K.release(m)

    def rope_tables(self, s, COS, SIN):
        K = self.K
        m = K.mark()
        posi = K.sb([64, S], I32)
        ang = K.sb([64, S], F32)
        kk = K.sb([64, S], F32)
        r = K.sb([64, S], F32)
        K.dma(posi, self.pos[s:s + 1, :].partition_broadcast(64), eng="sp")
        K.copy(ang, posi)
        K.ts(ang, ang, self.CST[0:64, INVF_OFF:INVF_OFF + 1], None, ALU.mult)
        K.ts(kk, ang, 1.0 / TWO_PI, MAGIC, ALU.mult, ALU.add)
        K.ts(kk, kk, MAGIC, None, ALU.subtract)
        K.stt(r, kk, -C1, ang, ALU.mult, ALU.add)
        K.stt(r, kk, -C2, r, ALU.mult, ALU.add)
        K.ts(ang, r, PI_LO, -PI_LO, ALU.min, ALU.max)
        K.act(SIN, ang, AF.Sin)
        K.ts(kk, r, np.pi / 2, -TWO_PI, ALU.is_gt, ALU.mult)
        K.stt(r, r, np.pi / 2, kk, ALU.add, ALU.add)
        K.ts(r, r, PI_LO, -PI_LO, ALU.min, ALU.max)
        K.act(COS, r, AF.Sin)
        K.release(m)

    def mla(self, s):
        K = self.K
        XT = self.XT
        HN = self.BIGA
        m0 = K.mark()
        self.set_banks(range(8))
        COS = K.sb([64, S], F32)
        SIN = K.sb([64, S], F32)
        self.rope_tables(s, COS, SIN)
        CQN = K.sb([P, 3, S], BF16)
        CKVN = K.sb([P, 2, S], BF16)
        KR = K.sb([P, S], BF16)
        K.memset(KR[64:128, :], 0.0)
        m1 = K.mark()
        WIN = K.sb([P, KC, 704], BF16)
        WKS = K.sb([P, KC, 64], BF16)
        K.dma(WIN, self.mla_w_in.rearrange("(c p) n -> p c n", p=P), eng="pool")
        K.ts(WKS[:, :, 0:32], WIN[:, :, 672:704], -1.0, None, ALU.mult)
        K.copy(WKS[:, :, 32:64], WIN[:, :, 640:672])
        tm = self.norm_tmps()
        self.prologue(0, S, G_OFF + 0 * 8, HN, tm)
        CQ = K.sb([P, 3, W], F32)
        T1 = K.sb([64, W], F32)
        T2 = K.sb([64, W], F32)
        SQ, SD, RS = tm
        for tb in range(S // W):
            cols = slice(tb * W, (tb + 1) * W)
            for (n, wof, gof, nf, dst) in ((3, 0, QN_OFF, QL, CQN), (2, QL, KVN_OFF, KVL, CKVN)):
                for c in range(n):
                    ps = K.ps(self.nb())
                    for kc in range(KC):
                        K.mm(ps, WIN[:, kc, wof + c * P: wof + (c + 1) * P], HN[:, kc, cols],
                             start=(kc == 0), stop=(kc == KC - 1))
                    K.copy(CQ[:, c, :], ps, eng="act")
                self.rstd(CQ[:, 0:n, :], n, W, nf, SQ, SD, RS, self.nb())
                for c in range(n):
                    K.stt(dst[:, c, cols], CQ[:, c, :], self.g(gof + c), RS, ALU.mult, ALU.mult)
            psa = K.ps(self.nb(), parts=64)
            psb = K.ps(self.nb(), parts=64)
            for kc in range(KC):
                K.mm(psa, WIN[:, kc, 640:704], HN[:, kc, cols], start=(kc == 0), stop=(kc == KC - 1))
            for kc in range(KC):
                K.mm(psb, WKS[:, kc, :], HN[:, kc, cols], start=(kc == 0), stop=(kc == KC - 1))
            K.tt(T1, psa, COS[:, cols], ALU.mult)
            K.tt(T2, psb, SIN[:, cols], ALU.mult)
            K.tt(KR[0:64, cols], T1, T2, ALU.add)
        K.release(m1)
        OALL = self.BIGA
        WQ = K.sb([P, 3, 1536], BF16)
        WQS = K.sb([P, 3, NH, 64], BF16)
        WKV = K.sb([P, 2, 2048], BF16)
        K.dma(WQ, self.mla_w_q_up.rearrange("(c p) n -> p c n", p=P), eng="pool")
        K.dma(WKV, self.mla_w_kv_up.rearrange("(c p) n -> p c n", p=P), eng="pool")
        WQv = WQ.rearrange("p c (h r) -> p c h r", r=192)
        K.ts(WQS[:, :, :, 0:32], WQv[:, :, :, 160:192], -1.0, None, ALU.mult)
        K.copy(WQS[:, :, :, 32:64], WQv[:, :, :, 128:160])
        QN = K.sb([P, S], BF16)
        QR = K.sb([P, S], BF16)
        K.memset(QR[64:128, :], 0.0)
        KN = K.sb([P, S], BF16)
        VH = K.sb([P, S // P, P], BF16)
        PT = [K.sb([P, W], BF16) for _ in range(3)]
        RZ = K.sb([P, W], F32)
        T1 = K.sb([64, W], F32)
        T2 = K.sb([64, W], F32)
        scale = float((128 + 64) ** -0.5)
        pj_banks = [0, 1, 2, 7]
        it = 0
        for h in range(NH):
            pj = 0
            for tb in range(S // W):
                cols = slice(tb * W, (tb + 1) * W)
                ps = K.ps(pj_banks[pj % 4]); pj += 1
                for c in range(3):
                    K.mm(ps, WQ[:, c, h * 192: h * 192 + 128], CQN[:, c, cols], start=(c == 0), stop=(c == 2))
                K.copy(QN[:, cols], ps, eng="dve")
                psa = K.ps(pj_banks[pj % 4], parts=64); pj += 1
                psb = K.ps(pj_banks[pj % 4], parts=64); pj += 1
                for c in range(3):
                    K.mm(psa, WQ[:, c, h * 192 + 128: h * 192 + 192], CQN[:, c, cols], start=(c == 0), stop=(c == 2))
                for c in range(3):
                    K.mm(psb, WQS[:, c, h, :], CQN[:, c, cols], start=(c == 0), stop=(c == 2))
                K.tt(T1, psa, COS[:, cols], ALU.mult)
                K.tt(T2, psb, SIN[:, cols], ALU.mult)
                K.tt(QR[0:64, cols], T1, T2, ALU.add)
                ps = K.ps(pj_banks[pj % 4]); pj += 1
                for c in range(2):
                    K.mm(ps, WKV[:, c, h * 256: h * 256 + 128], CKVN[:, c, cols], start=(c == 0), stop=(c == 1))
                K.copy(KN[:, cols], ps, eng="act")
            for kt in range(S // P):
                if kt % 4 == 0:
                    vb = pj_banks[pj % 4]; pj += 1
                ps = K.ps(vb, P, col0=(kt % 4) * P)
                for c in range(2):
                    K.mm(ps, CKVN[:, c, kt * P:(kt + 1) * P], WKV[:, c, h * 256 + 128: h * 256 + 256],
                         start=(c == 0), stop=(c == 1))
                if kt % 4 == 3:
                    K.copy(VH[:, kt - 3:kt + 1, :], K.ps(vb).rearrange("p (a b) -> p a b", a=4),
                           eng=("act" if (kt // 4) % 2 else "dve"))
            nkc = S // P
            nit = (S // W) * nkc

            def s_mm(i):
                qb_, kc_ = divmod(i, nkc)
                qc_ = slice(qb_ * W, (qb_ + 1) * W)
                kcs_ = slice(kc_ * P, (kc_ + 1) * P)
                pss_ = K.ps((0, 1, 2)[i % 3])
                K.mm(pss_, KN[:, kcs_], QN[:, qc_], start=True, stop=False)
                K.mm(pss_, KR[:, kcs_], QR[:, qc_], start=False, stop=True)

            s_mm(0)
            s_mm(1)
            for i in range(nit):
                qb, kc = divmod(i, nkc)
                qc = slice(qb * W, (qb + 1) * W)
                pso = K.ps(3 + (qb % 2))
                psz = K.ps(5 + (qb % 2))
                pss = K.ps((0, 1, 2)[i % 3])
                pt = PT[i % 3]
                K.act(pt, pss, AF.Exp, scale=scale)
                if i + 2 < nit:
                    s_mm(i + 2)
                K.mm(pso, VH[:, kc, :], pt, start=(kc == 0), stop=(kc == nkc - 1))
                K.mm(psz, self.ONESB, pt, start=(kc == 0), stop=(kc == nkc - 1))
                if kc == nkc - 1:
                    K.recip(RZ, psz)
                    K.tt(OALL[:, h, qc], pso, RZ, ALU.mult)
        K.release(m0)
        self.set_banks([0, 1, 2, 7])
        self.epilogue_full(OALL, self.mla_w_out, G_OFF + 1 * 8)

    def ffn(self, s, l):
        K = self.K
        XT = self.XT
        w_in_v = self.ffn_w_in[l].rearrange("(c p) n -> p c n", p=P)
        w_out_v = self.ffn_w_out[l].rearrange("(k p) n -> p k n", p=P)
        cw = lambda tap, j: self.g(CW_OFF + (l * 3 + tap) * NFF + j)
        cb = lambda j: self.g(CB_OFF + l * NFF + j)
        HL = S // 2
        mh = K.mark()
        HSAVE = K.sb([P, KC, 1], BF16)
        for hf in range(2):
            m0 = K.mark()
            t0 = hf * HL
            lo = t0
            hi = min(t0 + HL + 1, S)
            HNH, e1 = K.sb_at(self.BIGA_OFF, [P, KC, HL + 2], BF16)
            WOS0, e2 = K.sb_at(e1, [P, NFF, P], BF16)
            WOS1, e3 = K.sb_at(e2, [P, NFF, P], BF16)
            assert e3 <= self.BIGA_OFF + KC * S * 2
            ACTH = K.sb([P, NFF, HL], BF16)
            m1 = K.mark()
            tm = self.norm_tmps()
            self.set_banks([4, 5])
            self.prologue(lo, hi, G_OFF + (l * 4 + 2) * 8, HNH[:, :, lo - (t0 - 1):hi - (t0 - 1)], tm)
            if hf == 1:
                K.copy(HNH[:, :, 0:1], HSAVE)
            else:
                K.copy(HSAVE, HNH[:, :, HL:HL + 1])
            K.release(m1)
            WI = [K.sb([P, KC, 2, P], BF16) for _ in range(3)]
            Y = [K.sb([P, W], F32) for _ in range(2)]
            A = [K.sb([P, W], BF16) for _ in range(2)]

            def load(j):
                b = WI[j % 3]
                K.dma(b[:, :, 0, :], w_in_v[:, :, j * P:(j + 1) * P], eng="pool")
                K.dma(b[:, :, 1, :], w_in_v[:, :, DFF + j * P:DFF + (j + 1) * P], eng="pool")

            load(0)
            load(1)
            it = 0
            for j in range(NFF):
                if j + 2 < NFF:
                    load(j + 2)
                wi = WI[j % 3]
                for q in range(HL // W):
                    tq = t0 + q * W
                    ci = 1 + q * W
                    psg = K.ps(0 + (it % 2))
                    psv = K.ps(2 + (it % 2))
                    psh = K.ps(6 + (it % 2), 2)
                    y = Y[it % 2]
                    a = A[it % 2]
                    it += 1
                    for kc in range(KC):
                        K.mm(psg, wi[:, kc, 0, :], HNH[:, kc, ci:ci + W], start=(kc == 0), stop=(kc == KC - 1))
                    for kc in range(KC):
                        K.mm(psv, wi[:, kc, 1, :], HNH[:, kc, ci:ci + W], start=(kc == 0), stop=(kc == KC - 1))
                    has_l = tq > 0
                    has_r = tq + W < S
                    if has_l:
                        for kc in range(KC):
                            K.mm(psh[:, 0:1], wi[:, kc, 0, :], HNH[:, kc, ci - 1:ci],
                                 start=(kc == 0), stop=(kc == KC - 1))
                    if has_r:
                        for kc in range(KC):
                            K.mm(psh[:, 1:2], wi[:, kc, 0, :], HNH[:, kc, ci + W:ci + W + 1],
                                 start=(kc == 0), stop=(kc == KC - 1))
                    K.act(y, psg, AF.Identity, scale=cw(1, j), bias=cb(j))
                    K.stt(y[:, 1:W], psg[:, 0:W - 1], cw(0, j), y[:, 1:W], ALU.mult, ALU.add)
                    K.stt(y[:, 0:W - 1], psg[:, 1:W], cw(2, j), y[:, 0:W - 1], ALU.mult, ALU.add)
                    if has_l:
                        K.stt(y[:, 0:1], psh[:, 0:1], cw(0, j), y[:, 0:1], ALU.mult, ALU.add)
                    if has_r:
                        K.stt(y[:, W - 1:W], psh[:, 1:2], cw(2, j), y[:, W - 1:W], ALU.mult, ALU.add)
                    K.act(a, y, AF.Gelu_apprx_tanh)
                    K.tt(ACTH[:, j, q * W:(q + 1) * W], psv, a, ALU.mult)
            K.release(m1)
            WOS = [WOS0, WOS1]
            M2 = K.sb([P, HL // W, KC, W], F32)
            tm = self.norm_tmps()
            Tb = [K.sb([P, W], F32) for _ in range(2)]
            K.dma(WOS[0], w_out_v[:, :, 0:P], eng="pool")
            for dc in range(KC):
                if dc + 1 < KC:
                    K.dma(WOS[(dc + 1) % 2], w_out_v[:, :, (dc + 1) * P:(dc + 2) * P], eng="pool")
                wo = WOS[dc % 2]
                for bq in range(HL // W):
                    ps = K.ps(4 + ((dc * 2 + bq) % 2))
                    for k in range(NFF):
                        K.mm(ps, wo[:, k, :], ACTH[:, k, bq * W:(bq + 1) * W], start=(k == 0), stop=(k == NFF - 1))
                    K.copy(M2[:, bq, dc, :], ps, eng="act")
            self.set_banks([6, 7])
            for bq in range(HL // W):
                self.post_norm_residual(M2[:, bq], G_OFF + (l * 4 + 3) * 8, t0 + bq * W, tm, Tb)
            K.release(m0)
        K.release(mh)

    def hgrn(self, s):
        K = self.K
        XT = self.XT
        HN = self.BIGA
        NT = S // P
        m0 = K.mark()
        OB = K.sb([P, NH, S], BF16)
        m1 = K.mark()
        tm = self.norm_tmps()
        self.set_banks(range(8))
        self.prologue(0, S, G_OFF + (1 * 4 + 0) * 8, HN, tm)
        K.release(m1)
        w_in_v = self.hgrn_w_in.rearrange("(c p) n -> p c n", p=P)
        WH = K.sb([P, 5, KC, P], BF16)
        VHm = [K.sb([P, NT, P], BF16) for _ in range(2)]
        K.memset(VHm[0][64:128, :, :], 0.0)
        K.memset(VHm[1][0:64, :, :], 0.0)
        QT = [K.sb([P, S], BF16) for _ in range(2)]
        KTl = [K.sb([P, S], BF16) for _ in range(2)]
        REF = [K.sb([P, 32], F32) for _ in range(2)]
        LAM = [K.sb([P, 32], F32) for _ in range(2)]
        DLT = K.sb([P, 32], F32)
        qs_off = (K.top + CELL - 1) // CELL * CELL
        QS = K.sb([P, S], F32)
        mh = K.mark()
        qscale = float(128 ** -0.5)
        one_b = self.ONE[:, 0:1].to_broadcast([P, S])

        def load_wh(h, parts):
            for i in parts:
                K.dma(WH[:, i, :, :], w_in_v[:, :, i * D + h * P: i * D + (h + 1) * P], eng="pool")

        load_wh(0, range(5))
        for h in range(NH):
            K.release(mh)
            for tb in range(S // W):
                cols = slice(tb * W, (tb + 1) * W)
                ps = K.ps(6 + (tb % 2))
                for kc in range(KC):
                    K.mm(ps, WH[:, 0, kc, :], HN[:, kc, cols], start=(kc == 0), stop=(kc == KC - 1))
                K.act(QS[:, cols], ps, AF.Silu)
            for dr in range(2):
                lcol = dr * 8 + h
                K.release(mh)
                PP = K.sb([P, S], F32)
                KF = K.sb([P, S], F32)
                EE = K.sb([P, S // 2], F32)
                if dr == 0:
                    for tb in range(S // W):
                        cols = slice(tb * W, (tb + 1) * W)
                        ps = K.ps(6 + (tb % 2))
                        for kc in range(KC):
                            K.mm(ps, WH[:, 1, kc, :], HN[:, kc, cols], start=(kc == 0), stop=(kc == KC - 1))
                        K.act(PP[:, cols], ps, AF.Sigmoid)
                    for kt in range(NT):
                        ps = K.ps(kt // 4, P, col0=(kt % 4) * P)
                        for kc in range(KC):
                            K.mm(ps, HN[:, kc, kt * P:(kt + 1) * P], WH[:, 3, kc, :],
                                 start=(kc == 0), stop=(kc == KC - 1))
                    psbw = []
                    for tb in range(S // W):
                        cols = slice(tb * W, (tb + 1) * W)
                        ps = K.ps(4 + tb)
                        for kc in range(KC):
                            K.mm(ps, WH[:, 2, kc, :], HN[:, kc, cols], start=(kc == 0), stop=(kc == KC - 1))
                        psbw.append(ps)
                else:
                    for g_ in range(NT // 4):
                        pv = K.ps(g_).rearrange("p (a b) -> p a b", a=4)
                        K.copy(VHm[0][0:64, 4 * g_:4 * g_ + 4, :], pv[0:64], eng="dve")
                        K.copy(VHm[1][64:128, 4 * g_:4 * g_ + 4, :], pv[64:128], eng="dve")
                    for tb in range(S // W):
                        cols = slice(tb * W, (tb + 1) * W)
                        K.act(PP[:, cols], psbw[tb], AF.Sigmoid)
                K.ts(KF, PP, self.NOML[:, lcol:lcol + 1], self.OML[:, lcol:lcol + 1], ALU.mult, ALU.add)
                K.act(PP, PP, AF.Ln, scale=self.OML[:, lcol:lcol + 1], bias=self.LB[:, lcol:lcol + 1])
                K.scan(PP, one_b, PP, 0.0, ALU.mult, ALU.add)
                if dr == 1:
                    for hf_ in range(2):
                        hc_ = slice(hf_ * (S // 2), (hf_ + 1) * (S // 2))
                        K.act(EE, KF[:, hc_], AF.Ln, scale=-1.0, bias=self.ONE[:, 0:1])
                        K.tt(PP[:, hc_], EE, PP[:, hc_], ALU.subtract)
                PPv = PP.rearrange("p (n c) -> p n c", c=64)
                K.copy(REF[dr], PPv[:, :, 31 + dr])
                if dr == 0:
                    K.tt(DLT[:, 0:31], REF[dr][:, 1:32], REF[dr][:, 0:31], ALU.subtract)
                    K.act(LAM[dr][:, 0:31], DLT[:, 0:31], AF.Exp)
                else:
                    K.tt(DLT[:, 1:32], REF[dr][:, 0:31], REF[dr][:, 1:32], ALU.subtract)
                    K.act(LAM[dr][:, 1:32], DLT[:, 1:32], AF.Exp)
                K.tt(PPv, PPv, REF[dr].unsqueeze(2).to_broadcast([P, 32, 64]), ALU.subtract)
                for hf_ in range(2):
                    hc_ = slice(hf_ * (S // 2), (hf_ + 1) * (S // 2))
                    K.act(EE, PP[:, hc_], AF.Exp)
                    K.stt(QT[dr][:, hc_], QS[:, hc_], qscale, EE, ALU.mult, ALU.mult)
                for hf_ in range(2):
                    hc_ = slice(hf_ * (S // 2), (hf_ + 1) * (S // 2))
                    K.act(EE, PP[:, hc_], AF.Exp, scale=-1.0)
                    K.tt(KTl[dr][:, hc_], KF[:, hc_], EE, ALU.mult)
            K.release(mh)
            if h + 1 < NH:
                load_wh(h + 1, range(4))
            KTT = []
            off = qs_off
            for dr in range(2):
                v, off = K.sb_at(off, [P, NT, P], BF16)
                KTT.append(v)
            for dr in range(2):
                for kt in range(NT):
                    if kt % 8 == 0:
                        tbk = 6 + ((kt // 8) % 2)
                    K.transpose(K.ps(tbk, 64, dt=BF16, col0=(kt % 8) * 64), KTl[dr][:, kt * P:(kt + 1) * P], self.IDB)
                    if kt % 8 == 7:
                        K.copy(KTT[dr][:, kt - 7:kt + 1, :], K.ps(tbk, dt=BF16).rearrange("p (a b) -> p a b", a=8),
                               eng=("dve" if dr == 0 else "act"))
            OS = K.sb([P, S], F32)
            mo = K.mark()
            Tst = [[K.sb([P, P], F32) for _ in range(4)] for _ in range(2)]
            TBs = [[K.sb([P, P], BF16) for _ in range(8)] for _ in range(2)]
            order = [list(range(NT)), list(range(NT - 1, -1, -1))]
            chunk_seq = [[], []]
            for dr in range(2):
                for p_ in order[dr]:
                    chunk_seq[dr] += ([2 * p_, 2 * p_ + 1] if dr == 0 else [2 * p_ + 1, 2 * p_])
            tb_of = [{}, {}]
            st = [dict(ti=0, bi=0, prevT=None, cnt=0) for _ in range(2)]
            PSA = (0, 1)
            PSD = ((2, 3), (4, 5))
            PSO = (6, 7)

            def produce(dr, pi):
                p_ = order[dr][pi]
                cols = slice(p_ * P, (p_ + 1) * P)
                psa = K.ps(PSA[dr], P)
                K.mm(psa, KTl[dr][:, cols], QT[dr][:, cols])
                am = self.AM[dr * 2 + (pi % 2)]
                K.stt(am, psa, 3.0e38, self.MASK[dr], ALU.min, ALU.mult)
                pair = chunk_seq[dr][2 * pi: 2 * pi + 2]
                psds = []
                for ci_, n in enumerate(pair):
                    psd = K.ps(PSD[dr][pi % 2], P, col0=ci_ * P)
                    K.mm(psd, KTT[dr][:, p_, :], VHm[n % 2][:, p_, :])
                    psds.append(psd)
                sd = st[dr]
                for ci_, n in enumerate(pair):
                    psd = psds[ci_]
                    Tn = Tst[dr][sd["ti"] % 4]
                    sd["ti"] += 1
                    if sd["prevT"] is None:
                        K.copy(Tn, psd, eng="dve")
                    else:
                        lam_in = (n - 1) if dr == 0 else (n + 1)
                        K.stt(Tn, sd["prevT"], LAM[dr][:, lam_in:lam_in + 1], psd, ALU.mult, ALU.add)
                    sd["prevT"] = Tn
                    sd["cnt"] += 1
                    if sd["cnt"] < 2 * NT:
                        tbn = TBs[dr][sd["bi"] % 8]
                        sd["bi"] += 1
                        K.act(tbn, Tn, AF.Copy, scale=LAM[dr][:, n:n + 1])
                        tb_of[dr][n] = tbn

            def consume(dr, pi):
                p_ = order[dr][pi]
                cols = slice(p_ * P, (p_ + 1) * P)
                pso = K.ps(PSO[dr], P)
                am = self.AM[dr * 2 + (pi % 2)]
                mms = [(pso, VHm[0][:, p_, :], am), (pso, VHm[1][:, p_, :], am)]
                for n in chunk_seq[dr][2 * pi: 2 * pi + 2]:
                    hh = n % 2
                    pred = (n - 1) if dr == 0 else (n + 1)
                    if pred in tb_of[dr]:
                        mms.append((pso[:, 64 * hh:64 * hh + 64], tb_of[dr][pred],
                                    QT[dr][:, p_ * P + 64 * hh: p_ * P + 64 * hh + 64]))
                for i_, (o_, l_, r_) in enumerate(mms):
                    K.mm(o_, l_, r_, start=(i_ == 0), stop=(i_ == len(mms) - 1))
                first = (dr == 0 and p_ < NT // 2) or (dr == 1 and p_ >= NT // 2)
                if first:
                    K.copy(OS[:, cols], pso, eng="act")
                else:
                    K.tt(OS[:, cols], pso, OS[:, cols], ALU.add)

            for dr in range(2):
                produce(dr, 0)
            for pi in range(NT):
                for dr in range(2):
                    if pi + 1 < NT:
                        produce(dr, pi + 1)
                    consume(dr, pi)
            K.release(mo)
            SG = K.sb([P, S], F32)
            SQ1 = K.sb([P, 2, W], BF16)
            SD = K.sb([P, 2 * W], F32)
            for tb in range(S // W):
                cols = slice(tb * W, (tb + 1) * W)
                ps = K.ps(2 + (tb % 2))
                for kc in range(KC):
                    K.mm(ps, WH[:, 4, kc, :], HN[:, kc, cols], start=(kc == 0), stop=(kc == KC - 1))
                K.act(SG[:, cols], ps, AF.Silu)
            for hf in range(2):
                hc = slice(hf * 2 * W, (hf + 1) * 2 * W)
                K.act(SQ1, OS[:, hc].rearrange("p (a b) -> p a b", a=2), AF.Square)
                pst = K.psum[:, 4 * 512: 6 * 512]
                for a_ in range(2):
                    K.mm(K.ps(4 + a_), self.ONESB, SQ1[:, a_, :])
                K.act(SD, pst, AF.Ln, scale=1.0 / P, bias=self.EPS[:, 0:1])
                K.act(SD, SD, AF.Exp, scale=-0.5)
                K.stt(OS[:, hc], OS[:, hc], self.g(ONG_OFF + h), SD, ALU.mult, ALU.mult)
                K.tt(OB[:, h, hc], OS[:, hc], SG[:, hc], ALU.mult)
            if h + 1 < NH:
                load_wh(h + 1, [4])
        K.release(m0)
        OBk = K.sb([P, NH, S], BF16)
        self.set_banks([0, 1, 2, 7])
        self.epilogue_full(OBk, self.hgrn_w_out, G_OFF + (1 * 4 + 1) * 8)
        K.release(m0)


def _cols(v):
    v = np.asarray(v, dtype=np.float32)
    return np.ascontiguousarray(v.reshape(-1, P).T)


def make_consts(pre_mix_norm, post_mix_norm, pre_ffn_norm, post_ffn_norm, mla_q_norm, mla_kv_norm,
                hgrn_out_norm, hgrn_lb_logits, ffn_conv_w, ffn_conv_b):
    c = np.zeros((P, NCST), np.float32)
    for l in range(2):
        for kind, arr in enumerate((pre_mix_norm, post_mix_norm, pre_ffn_norm, post_ffn_norm)):
            o = G_OFF + (l * 4 + kind) * 8
            c[:, o:o + 8] = _cols(arr[l])
    c[:, QN_OFF:QN_OFF + 3] = _cols(mla_q_norm[0])
    c[:, KVN_OFF:KVN_OFF + 2] = _cols(mla_kv_norm[0])
    c[:, ONG_OFF:ONG_OFF + 8] = _cols(hgrn_out_norm[0])
    for d in range(2):
        for l in range(2):
            o = LBL_OFF + (d * 2 + l) * 8
            c[:, o:o + 8] = _cols(hgrn_lb_logits[d, l])
    for l in range(2):
        for tap in range(3):
            o = CW_OFF + (l * 3 + tap) * NFF
            c[:, o:o + NFF] = _cols(ffn_conv_w[l, tap])
        o = CB_OFF + l * NFF
        c[:, o:o + NFF] = _cols(ffn_conv_b[l])
    inv = (1.0 / (np.float32(10000.0) ** (np.arange(0, 64, 2, dtype=np.float32) / np.float32(64)))).astype(np.float32)
    c[0:32, INVF_OFF] = inv
    c[32:64, INVF_OFF] = inv
    return c


_PROG_CACHE = {}


def get_prog(nseq=NSEQ, stages=("mla", "ffn0", "hgrn", "ffn1")):
    key = (nseq, tuple(stages))
    if key not in _PROG_CACHE:
        _PROG_CACHE[key] = Prog(nseq, stages)
    return _PROG_CACHE[key]


def core_inputs(x_c, pos_c, cst, w):
    m = {"x": np.ascontiguousarray(x_c, dtype=np.float32),
         "pos": np.ascontiguousarray(pos_c, dtype=np.int32),
         "cst": cst}
    m.update(w)
    return m


def weight_map(mla_w_in, mla_w_q_up, mla_w_kv_up, mla_w_out, hgrn_w_in, hgrn_w_out, ffn_w_in, ffn_w_out):
    f = lambda a: np.ascontiguousarray(np.asarray(a, dtype=np.float32))
    return {"mla_w_in": f(mla_w_in[0]), "mla_w_q_up": f(mla_w_q_up[0]), "mla_w_kv_up": f(mla_w_kv_up[0]),
            "mla_w_out": f(mla_w_out[0]), "hgrn_w_in": f(hgrn_w_in[0]), "hgrn_w_out": f(hgrn_w_out[0]),
            "ffn_w_in": f(ffn_w_in), "ffn_w_out": f(ffn_w_out)}


def kernel(x, positions, pre_mix_norm, post_mix_norm, pre_ffn_norm, post_ffn_norm,
           mla_w_in, mla_q_norm, mla_w_q_up, mla_kv_norm, mla_w_kv_up, mla_w_out,
           hgrn_w_in, hgrn_lb_logits, hgrn_out_norm, hgrn_w_out,
           ffn_w_in, ffn_conv_w, ffn_conv_b, ffn_w_out):
    n_cores = 8
    x = np.asarray(x)
    positions = np.asarray(positions)
    cst = make_consts(np.asarray(pre_mix_norm), np.asarray(post_mix_norm), np.asarray(pre_ffn_norm),
                      np.asarray(post_ffn_norm), np.asarray(mla_q_norm), np.asarray(mla_kv_norm),
                      np.asarray(hgrn_out_norm), np.asarray(hgrn_lb_logits), np.asarray(ffn_conv_w),
                      np.asarray(ffn_conv_b))
    w = weight_map(mla_w_in, mla_w_q_up, mla_w_kv_up, mla_w_out, hgrn_w_in, hgrn_w_out, ffn_w_in, ffn_w_out)
    prog = get_prog()
    in_maps = [core_inputs(x[c * NSEQ:(c + 1) * NSEQ], positions[c * NSEQ:(c + 1) * NSEQ], cst, w)
               for c in range(n_cores)]
    res = run_bass_kernel_spmd(prog.nc, in_maps, core_ids=list(range(n_cores)))
    out = np.concatenate([np.asarray(r["out"]) for r in res.results], axis=0)
    return out.astype(np.float32, copy=False)
```

```python
import contextlib
import numpy as np
import concourse.bass as bass
import concourse.mybir as mybir
from concourse.bass_utils import run_bass_kernel_spmd

F32 = mybir.dt.float32
BF16 = mybir.dt.bfloat16
I32 = mybir.dt.int32
U8 = mybir.dt.uint8
AF = mybir.ActivationFunctionType
ALU = mybir.AluOpType

COMPUTE = ("pe", "act", "dve", "pool")
ALLENG = COMPUTE + ("sp",)
NDMA_SEMS = 24
CELL = 512
ARENA_BYTES = 206 * 1024
DEBUG_WHERE = False


def _dsize(dt):
    return mybir.dt.size(dt)


class Sched:
    def __init__(self, nc):
        self.nc = nc
        self.ops = []
        self.last_writer = {}
        self.readers = {}
        self.n_dma = 0
        self._cache = {}
        self.names = {}

    def cells(self, a):
        if isinstance(a, str):
            return (a,)
        t = a.tensor
        key = (t.name, a.offset, a.ap, str(a.dtype))
        r = self._cache.get(key)
        if r is not None:
            return r
        nm = t.name
        if nm.startswith("arena"):
            sp = "S"
        elif nm.startswith("psum"):
            sp = "P"
        else:
            r = ()
            self._cache[key] = r
            return r
        ds = _dsize(a.dtype)
        ap = a.ap
        pstride, pcount = ap[0]
        if pstride > 0:
            p0 = a.offset // pstride
            f0 = a.offset % pstride
        else:
            p0 = 0
            f0 = a.offset
        free = ap[1:]
        starts = [f0]
        for (st, cnt) in free[:-1]:
            starts = [s + i * st for s in starts for i in range(cnt)]
        if free:
            lst, lcnt = free[-1]
            span = (lcnt - 1) * abs(lst) + 1
        else:
            span = 1
        pg0 = p0 // 32
        pg1 = (p0 + pcount - 1) // 32
        cs = set()
        cell = CELL if sp == "S" else 2048
        for s in starts:
            b0 = (s * ds) // cell
            b1 = ((s + span) * ds - 1) // cell
            for b in range(b0, b1 + 1):
                for pg in range(pg0, pg1 + 1):
                    cs.add((sp, pg, b))
        r = tuple(cs)
        self._cache[key] = r
        return r

    def op(self, eng, fn, reads=(), writes=(), dma=False):
        idx = len(self.ops)
        deps = {}
        rc = set()
        wc = set()
        for a in reads:
            rc.update(self.cells(a))
        for a in writes:
            wc.update(self.cells(a))
        for k in rc:
            w = self.last_writer.get(k)
            if w is not None:
                deps[w] = True
        for k in wc:
            w = self.last_writer.get(k)
            if w is not None and w not in deps:
                deps[w] = False
            rl = self.readers.get(k)
            if rl:
                for r in rl:
                    if r not in deps:
                        deps[r] = False
        for k in rc:
            if k in wc:
                continue
            rl = self.readers.get(k)
            if rl is None:
                self.readers[k] = [idx]
            else:
                rl.append(idx)
        for k in wc:
            self.last_writer[k] = idx
            self.readers[k] = None
        rec = dict(eng=eng, fn=fn, deps=deps, dma=dma, signal=False, dma_id=None)
        if DEBUG_WHERE:
            import sys as _sys
            f = _sys._getframe(1)
            w = []
            while f is not None and len(w) < 5:
                w.append("%s:%d" % (f.f_code.co_name, f.f_lineno))
                f = f.f_back
            rec["where"] = w
        if dma:
            rec["dma_id"] = self.n_dma
            self.n_dma += 1
        self.ops.append(rec)
        return idx

    def emit(self, final_wait_eng="sp"):
        nc = self.nc
        ops = self.ops
        dma_ops = [i for i, o in enumerate(ops) if o["dma"]]
        for i, o in enumerate(ops):
            nd = {}
            for d, raw in o["deps"].items():
                p = ops[d]
                if (not p["dma"]) and (not o["dma"]) and p["eng"] == o["eng"] == "pe":
                    continue
                nd[d] = raw
            if o["dma"] and o["dma_id"] >= NDMA_SEMS:
                nd[dma_ops[o["dma_id"] - NDMA_SEMS]] = True
            o["deps"] = nd
            for d in nd:
                ops[d]["signal"] = True
        ms = {e: 0 for e in ALLENG}
        for o in ops:
            if o["dma"]:
                o["sig"] = (("dma", o["dma_id"] % NDMA_SEMS), 16 * (o["dma_id"] // NDMA_SEMS + 1))
            elif o["signal"]:
                ms[o["eng"]] += 1
                o["sig"] = (("eng", o["eng"]), ms[o["eng"]])
        with contextlib.ExitStack() as stack:
            sems = {}
            for e in ALLENG:
                sems[("eng", e)] = stack.enter_context(nc.semaphore("s_" + e))
            for j in range(min(NDMA_SEMS, max(self.n_dma, 1))):
                sems[("dma", j)] = stack.enter_context(nc.semaphore("s_dma%d" % j))
            block = stack.enter_context(nc.Block())
            final_dma = {}
            for o in ops:
                if o["dma"]:
                    k, v = o["sig"]
                    final_dma[k] = max(final_dma.get(k, 0), v)

            def make(engname):
                def body(eng):
                    waited = {}
                    for o in ops:
                        if o["eng"] != engname:
                            continue
                        need = {}
                        for d in o["deps"]:
                            k, v = ops[d]["sig"]
                            if v > need.get(k, 0):
                                need[k] = v
                        for k, v in need.items():
                            if waited.get(k, 0) >= v:
                                continue
                            eng.wait_ge(sems[k], v)
                            waited[k] = v
                        ins = o["fn"](eng)
                        if DEBUG_WHERE:
                            try:
                                self.names[str(ins.ins.name)] = o["where"]
                            except Exception:
                                pass
                        if o["dma"]:
                            k, v = o["sig"]
                            ins.then_inc(sems[k], 16)
                        elif o["signal"]:
                            ins.then_inc(sems[("eng", engname)], 1)
                    if engname == final_wait_eng:
                        for k, v in final_dma.items():
                            if waited.get(k, 0) < v:
                                eng.wait_ge(sems[k], v)
                return body

            used = set(o["eng"] for o in ops) | {final_wait_eng}
            if "pe" in used:
                block.tensor(make("pe"))
            if "act" in used:
                block.scalar(make("act"))
            if "dve" in used:
                block.vector(make("dve"))
            if "pool" in used:
                block.gpsimd(make("pool"))
            if "sp" in used:
                block.sync(make("sp"))


class KB:
    def __init__(self, nc, stack):
        self.nc = nc
        self.S = Sched(nc)
        self.arena = stack.enter_context(nc.sbuf_tensor("arena", [128, ARENA_BYTES], U8))
        self.psum = stack.enter_context(nc.psum_tensor("psum", [128, 4096], F32))
        self.top = 0

    def sb(self, shape, dt, align=CELL):
        n = int(np.prod(shape[1:])) * _dsize(dt)
        off = (self.top + align - 1) // align * align
        assert off + n <= ARENA_BYTES, "SBUF arena overflow: need %d" % (off + n)
        self.top = off + n
        v = self.arena[0:shape[0], off:off + n].bitcast(dt)
        if len(shape) == 3:
            v = v.rearrange("p (a b) -> p a b", a=shape[1])
        elif len(shape) == 4:
            v = v.rearrange("p (a b c) -> p a b c", a=shape[1], b=shape[2])
        return v

    def sb_at(self, off, shape, dt):
        save = self.top
        self.top = off
        v = self.sb(shape, dt)
        end = self.top
        self.top = save
        return v, end

    def mark(self):
        return self.top

    def release(self, m):
        self.top = m

    def ps(self, bank, width=512, dt=F32, parts=128, col0=0):
        v = self.psum[0:parts, bank * 512 + col0: bank * 512 + col0 + width]
        if dt != F32:
            v = v.bitcast(dt)
        return v

    def dma(self, out, in_, eng="sp", extra_reads=(), extra_writes=()):
        return self.S.op(eng, lambda e: e.dma_start(out=out, in_=in_),
                         reads=[in_] + list(extra_reads), writes=[out] + list(extra_writes), dma=True)

    def mm(self, out, lhsT, rhs, start=True, stop=True):
        return self.S.op("pe", lambda e: e.matmul(out, lhsT, rhs, start=start, stop=stop),
                         reads=[lhsT, rhs], writes=[out])

    def transpose(self, out, in_, ident):
        return self.S.op("pe", lambda e: e.transpose(out, in_, ident), reads=[in_, ident], writes=[out])

    def act(self, out, in_, func, scale=1.0, bias=None, eng="act"):
        reads = [in_]
        if not isinstance(scale, (int, float)):
            reads.append(scale)
        kw = {}
        if bias is not None:
            kw["bias"] = bias
            if not isinstance(bias, (int, float)):
                reads.append(bias)
        return self.S.op("act", lambda e: e.activation(out=out, in_=in_, func=func, scale=scale, **kw),
                         reads=reads, writes=[out])

    def tt(self, out, in0, in1, op, eng="dve"):
        return self.S.op(eng, lambda e: e.tensor_tensor(out=out, in0=in0, in1=in1, op=op),
                         reads=[in0, in1], writes=[out])

    def ts(self, out, in0, s1, s2, op0, op1=None, eng="dve"):
        reads = [in0]
        for s in (s1, s2):
            if s is not None and not isinstance(s, (int, float)):
                reads.append(s)
        if op1 is None:
            f = lambda e: e.tensor_scalar(out=out, in0=in0, scalar1=s1, scalar2=None, op0=op0)
        else:
            f = lambda e: e.tensor_scalar(out=out, in0=in0, scalar1=s1, scalar2=s2, op0=op0, op1=op1)
        return self.S.op(eng, f, reads=reads, writes=[out])

    def stt(self, out, in0, scalar, in1, op0, op1, eng="dve"):
        reads = [in0, in1]
        if not isinstance(scalar, (int, float)):
            reads.append(scalar)
        return self.S.op(eng, lambda e: e.scalar_tensor_tensor(out=out, in0=in0, scalar=scalar, in1=in1,
                                                               op0=op0, op1=op1),
                         reads=reads, writes=[out])

    def copy(self, out, in_, eng="dve"):
        if eng == "act":
            return self.act(out, in_, AF.Copy)
        return self.S.op(eng, lambda e: e.tensor_copy(out=out, in_=in_), reads=[in_], writes=[out])

    def recip(self, out, in_):
        return self.S.op("dve", lambda e: e.reciprocal(out=out, in_=in_), reads=[in_], writes=[out])

    def memset(self, out, val, eng="pool"):
        return self.S.op(eng, lambda e: e.memset(out, val), writes=[out])

    def scan(self, out, d0, d1, init, op0, op1):
        reads = [d0, d1]
        if not isinstance(init, (int, float)):
            reads.append(init)
        return self.S.op("dve", lambda e: e.tensor_tensor_scan(out=out, data0=d0, data1=d1, initial=init,
                                                               op0=op0, op1=op1),
                         reads=reads, writes=[out])

    def copy_pred(self, out, mask, data):
        return self.S.op("dve", lambda e: e.copy_predicated(out=out, mask=mask, data=data),
                         reads=[mask, data, out], writes=[out])

    def affine_select(self, out, in_, pattern, cmp, fill, base, cm):
        return self.S.op("pool", lambda e: e.affine_select(out=out, in_=in_, pattern=pattern, compare_op=cmp,
                                                           fill=fill, base=base, channel_multiplier=cm),
                         reads=[in_], writes=[out])


P = 128
D = 1024
KC = 8
S = 2048
W = 512
NSEQ = 2
NH = 8
DFF = 2816
NFF = 22
QL, KVL = 384, 256
EPSV = 1e-6
TWO_PI = 6.283185307179586
MAGIC = 12582912.0
C1 = 6.28125
C2 = TWO_PI - C1
PI_LO = 3.1415925

G_OFF = 0
QN_OFF = 64
KVN_OFF = 67
ONG_OFF = 69
LBL_OFF = 77
CW_OFF = 109
CB_OFF = 241
INVF_OFF = 285
NCST = 286


class Prog:
    def __init__(self, nseq=NSEQ, stages=("mla", "ffn0", "hgrn", "ffn1"), dbg=None):
        self.nseq = nseq
        self.stages = stages
        self.dbg = dbg
        nc = bass.Bass("TRN2", target_bir_lowering=False)
        self.nc = nc
        dt = nc.dram_tensor
        self.x = dt("x", [nseq, S, D], F32, kind="ExternalInput").ap()
        self.pos = dt("pos", [nseq, S], I32, kind="ExternalInput").ap()
        self.cst = dt("cst", [P, NCST], F32, kind="ExternalInput").ap()
        self.mla_w_in = dt("mla_w_in", [D, 704], F32, kind="ExternalInput").ap()
        self.mla_w_q_up = dt("mla_w_q_up", [QL, 1536], F32, kind="ExternalInput").ap()
        self.mla_w_kv_up = dt("mla_w_kv_up", [KVL, 2048], F32, kind="ExternalInput").ap()
        self.mla_w_out = dt("mla_w_out", [D, D], F32, kind="ExternalInput").ap()
        self.hgrn_w_in = dt("hgrn_w_in", [D, 5120], F32, kind="ExternalInput").ap()
        self.hgrn_w_out = dt("hgrn_w_out", [D, D], F32, kind="ExternalInput").ap()
        self.ffn_w_in = dt("ffn_w_in", [2, D, 2 * DFF], F32, kind="ExternalInput").ap()
        self.ffn_w_out = dt("ffn_w_out", [2, DFF, D], F32, kind="ExternalInput").ap()
        self.out = dt("out", [nseq, S, D], F32, kind="ExternalOutput").ap()
        with contextlib.ExitStack() as st:
            self.K = KB(nc, st)
            self.setup()
            for s in range(nseq):
                self.sequence(s)
            self.K.S.emit()

    def nb(self):
        b = self._bank
        self._bank = (self._bank + 1) % len(self._banks)
        return self._banks[b]

    def set_banks(self, banks):
        self._banks = list(banks)
        self._bank = 0

    def g(self, col):
        return self.CST[:, col:col + 1]

    def setup(self):
        K = self.K
        self.set_banks(range(8))
        self.CST = K.sb([P, NCST], F32)
        K.dma(self.CST, self.cst)
        self.IDF = K.sb([P, P], F32)
        K.memset(self.IDF, 0.0)
        K.affine_select(self.IDF, self.IDF, [[-1, P]], ALU.not_equal, 1.0, 0, 1)
        self.IDB = K.sb([P, P], BF16)
        K.copy(self.IDB, self.IDF)
        self.ONESB = K.sb([P, P], BF16)
        K.memset(self.ONESB, 1.0)
        self.EPS = K.sb([P, 1], F32)
        K.memset(self.EPS, EPSV)
        self.ONE = K.sb([P, 1], F32)
        K.memset(self.ONE, 1.0)
        self.MASK = []
        for dr in range(2):
            mf = K.sb([P, P], F32)
            K.memset(mf, 1.0)
            if dr == 0:
                K.affine_select(mf, mf, [[1, P]], ALU.is_ge, 0.0, 0, -1)
                K.memset(mf[0:64, 64:128], 0.0)
            else:
                K.affine_select(mf, mf, [[-1, P]], ALU.is_ge, 0.0, 0, 1)
                K.memset(mf[64:128, 0:64], 0.0)
            self.MASK.append(mf)
        self.AM = [K.sb([P, P], BF16) for _ in range(4)]
        for a in self.AM:
            K.memset(a, 0.0)
        self.LB = K.sb([P, 16], F32)
        self.OML = K.sb([P, 16], F32)
        self.NOML = K.sb([P, 16], F32)
        lbl = self.CST[:, LBL_OFF:LBL_OFF + 32].rearrange("p (d l h) -> p d l h", d=2, l=2)
        dv = K.sb([P, 2, 8], F32)
        K.tt(dv, lbl[:, :, 1, :], lbl[:, :, 0, :], ALU.subtract)
        K.act(self.LB.rearrange("p (d h) -> p d h", d=2), dv, AF.Sigmoid)
        K.ts(self.OML, self.LB, -1.0, 1.0, ALU.mult, ALU.add)
        K.ts(self.NOML, self.LB, -1.0, None, ALU.add)
        self.XT = K.sb([P, KC, S], F32)
        self.BIGA_OFF = (K.top + CELL - 1) // CELL * CELL
        self.BIGA = K.sb([P, KC, S], BF16)

    def rstd(self, src3, n, width, nfeat, SQ, SD, RS, bank):
        K = self.K
        K.act(SQ[:, 0:n, 0:width], src3, AF.Square)
        ps = K.ps(bank, width)
        for c in range(n):
            K.mm(ps, self.ONESB, SQ[:, c, 0:width], start=(c == 0), stop=(c == n - 1))
        K.act(SD[:, 0:width], ps, AF.Ln, scale=1.0 / nfeat, bias=self.EPS[:, 0:1])
        K.act(RS[:, 0:width], SD[:, 0:width], AF.Exp, scale=-0.5)

    def norm_tmps(self):
        K = self.K
        return K.sb([P, KC, W], BF16), K.sb([P, W], F32), K.sb([P, W], F32)

    def prologue(self, t0, t1, gcol, dst, tm):
        K = self.K
        SQ, SD, RS = tm
        b = t0
        while b < t1:
            e = min(b + W, t1)
            w = e - b
            self.rstd(self.XT[:, :, b:e], KC, w, D, SQ, SD, RS, self.nb())
            for c in range(KC):
                K.stt(dst[:, c, b - t0:e - t0], self.XT[:, c, b:e], self.g(gcol + c), RS[:, 0:w],
                      ALU.mult, ALU.mult)
            b = e

    def post_norm_residual(self, M3, gcol, t0, tm, Tb):
        K = self.K
        SQ, SD, RS = tm
        self.rstd(M3, KC, W, D, SQ, SD, RS, self.nb())
        for dc in range(KC):
            T = Tb[dc % 2]
            K.stt(T, M3[:, dc, :], self.g(gcol + dc), RS, ALU.mult, ALU.mult)
            K.tt(self.XT[:, dc, t0:t0 + W], self.XT[:, dc, t0:t0 + W], T, ALU.add, eng="pool")

    def epilogue_full(self, Y, wdram, gcol):
        K = self.K
        m = K.mark()
        WO = K.sb([P, KC, D], BF16)
        K.dma(WO, wdram.rearrange("(k p) n -> p k n", p=P), eng="pool")
        M = K.sb([P, KC, W], F32)
        tm = self.norm_tmps()
        Tb = [K.sb([P, W], F32) for _ in range(2)]
        for tb in range(S // W):
            cols = slice(tb * W, (tb + 1) * W)
            for dc in range(KC):
                ps = K.ps(self.nb())
                for k in range(KC):
                    K.mm(ps, WO[:, k, dc * P:(dc + 1) * P], Y[:, k, cols], start=(k == 0), stop=(k == KC - 1))
                K.copy(M[:, dc, :], ps, eng="act")
            self.post_norm_residual(M, gcol, tb * W, tm, Tb)
        K.release(m)

    def sequence(self, s):
        self.set_banks(range(8))
        self.load_x(s)
        if "mla" in self.stages:
            self.mla(s)
        if "ffn0" in self.stages:
            self.ffn(s, 0)
        if "hgrn" in self.stages:
            self.hgrn(s)
        if "ffn1" in self.stages:
            self.ffn(s, 1)
        self.store_x(s)

    def load_x(self, s):
        K = self.K
        m = K.mark()
        XIN = [K.sb([P, D], F32) for _ in range(2)]
        self.set_banks(range(8))
        for t in range(S // P):
            xin = XIN[t % 2]
            K.dma(xin, self.x[s, t * P:(t + 1) * P, :], eng="sp")
            for half in range(2):
                bank = self.nb()
                for cc in range(4):
                    c = half * 4 + cc
                    K.transpose(K.ps(bank, P, col0=cc * P), xin[:, c * P:(c + 1) * P], self.IDF)
                K.copy(self.XT[:, half * 4:(half + 1) * 4, t * P:(t + 1) * P],
                       K.ps(bank).rearrange("p (a b) -> p a b", a=4), eng=("act" if half else "dve"))
        K.release(m)

    def store_x(self, s):
        K = self.K
        m = K.mark()
        XO = [K.sb([P, D], F32) for _ in range(2)]
        self.set_banks(range(8))
        for t in range(S // P):
            xo = XO[t % 2]
            for half in range(2):
                bank = self.nb()
                for cc in range(4):
                    c = half * 4 + cc
                    K.transpose(K.ps(bank, P, col0=cc * P), self.XT[:, c, t * P:(t + 1) * P], self.IDF)
                K.copy(xo[:, half * W:(half + 1) * W], K.ps(bank), eng=("act" if half else "dve"))
            K.dma(self.out[s, t * P:(t + 1) * P, :], xo, eng="sp")
        K.release(m)

    def rope_tables(self, s, COS, SIN):
        K = self.K
        m = K.mark()
        posi = K.sb([64, S], I32)
        ang = K.sb([64, S], F32)
        kk = K.sb([64, S], F32)
        r = K.sb([64, S], F32)
        K.dma(posi, self.pos[s:s + 1, :].partition_broadcast(64), eng="sp")
        K.copy(ang, posi)
        K.ts(ang, ang, self.CST[0:64, INVF_OFF:INVF_OFF + 1], None, ALU.mult)
        K.ts(kk, ang, 1.0 / TWO_PI, MAGIC, ALU.mult, ALU.add)
        K.ts(kk, kk, MAGIC, None, ALU.subtract)
        K.stt(r, kk, -C1, ang, ALU.mult, ALU.add)
        K.stt(r, kk, -C2, r, ALU.mult, ALU.add)
        K.ts(ang, r, PI_LO, -PI_LO, ALU.min, ALU.max)
        K.act(SIN, ang, AF.Sin)
        K.ts(kk, r, np.pi / 2, -TWO_PI, ALU.is_gt, ALU.mult)
        K.stt(r, r, np.pi / 2, kk, ALU.add, ALU.add)
        K.ts(r, r, PI_LO, -PI_LO, ALU.min, ALU.max)
        K.act(COS, r, AF.Sin)
        K.release(m)

    def mla(self, s):
        K = self.K
        XT = self.XT
        HN = self.BIGA
        m0 = K.mark()
        self.set_banks(range(8))
        COS = K.sb([64, S], F32)
        SIN = K.sb([64, S], F32)
        self.rope_tables(s, COS, SIN)
        CQN = K.sb([P, 3, S], BF16)
        CKVN = K.sb([P, 2, S], BF16)
        KR = K.sb([P, S], BF16)
        K.memset(KR[64:128, :], 0.0)
        m1 = K.mark()
        WIN = K.sb([P, KC, 704], BF16)
        WKS = K.sb([P, KC, 64], BF16)
        K.dma(WIN, self.mla_w_in.rearrange("(c p) n -> p c n", p=P), eng="pool")
        K.ts(WKS[:, :, 0:32], WIN[:, :, 672:704], -1.0, None, ALU.mult)
        K.copy(WKS[:, :, 32:64], WIN[:, :, 640:672])
        tm = self.norm_tmps()
        self.prologue(0, S, G_OFF + 0 * 8, HN, tm)
        CQ = K.sb([P, 3, W], F32)
        T1 = K.sb([64, W], F32)
        T2 = K.sb([64, W], F32)
        SQ, SD, RS = tm
        for tb in range(S // W):
            cols = slice(tb * W, (tb + 1) * W)
            for (n, wof, gof, nf, dst) in ((3, 0, QN_OFF, QL, CQN), (2, QL, KVN_OFF, KVL, CKVN)):
                for c in range(n):
                    ps = K.ps(self.nb())
                    for kc in range(KC):
                        K.mm(ps, WIN[:, kc, wof + c * P: wof + (c + 1) * P], HN[:, kc, cols],
                             start=(kc == 0), stop=(kc == KC - 1))
                    K.copy(CQ[:, c, :], ps, eng="act")
                self.rstd(CQ[:, 0:n, :], n, W, nf, SQ, SD, RS, self.nb())
                for c in range(n):
                    K.stt(dst[:, c, cols], CQ[:, c, :], self.g(gof + c), RS, ALU.mult, ALU.mult)
            psa = K.ps(self.nb(), parts=64)
            psb = K.ps(self.nb(), parts=64)
            for kc in range(KC):
                K.mm(psa, WIN[:, kc, 640:704], HN[:, kc, cols], start=(kc == 0), stop=(kc == KC - 1))
            for kc in range(KC):
                K.mm(psb, WKS[:, kc, :], HN[:, kc, cols], start=(kc == 0), stop=(kc == KC - 1))
            K.tt(T1, psa, COS[:, cols], ALU.mult)
            K.tt(T2, psb, SIN[:, cols], ALU.mult)
            K.tt(KR[0:64, cols], T1, T2, ALU.add)
        K.release(m1)
        OALL = self.BIGA
        WQ = K.sb([P, 3, 1536], BF16)
        WQS = K.sb([P, 3, NH, 64], BF16)
        WKV = K.sb([P, 2, 2048], BF16)
        K.dma(WQ, self.mla_w_q_up.rearrange("(c p) n -> p c n", p=P), eng="pool")
        K.dma(WKV, self.mla_w_kv_up.rearrange("(c p) n -> p c n", p=P), eng="pool")
        WQv = WQ.rearrange("p c (h r) -> p c h r", r=192)
        K.ts(WQS[:, :, :, 0:32], WQv[:, :, :, 160:192], -1.0, None, ALU.mult)
        K.copy(WQS[:, :, :, 32:64], WQv[:, :, :, 128:160])
        QN = K.sb([P, S], BF16)
        QR = K.sb([P, S], BF16)
        K.memset(QR[64:128, :], 0.0)
        KN = K.sb([P, S], BF16)
        VH = K.sb([P, S // P, P], BF16)
        PT = [K.sb([P, W], BF16) for _ in range(3)]
        RZ = K.sb([P, W], F32)
        T1 = K.sb([64, W], F32)
        T2 = K.sb([64, W], F32)
        scale = float((128 + 64) ** -0.5)
        pj_banks = [0, 1, 2, 7]
        it = 0
        for h in range(NH):
            pj = 0
            for tb in range(S // W):
                cols = slice(tb * W, (tb + 1) * W)
                ps = K.ps(pj_banks[pj % 4]); pj += 1
                for c in range(3):
                    K.mm(ps, WQ[:, c, h * 192: h * 192 + 128], CQN[:, c, cols], start=(c == 0), stop=(c == 2))
                K.copy(QN[:, cols], ps, eng="dve")
                psa = K.ps(pj_banks[pj % 4], parts=64); pj += 1
                psb = K.ps(pj_banks[pj % 4], parts=64); pj += 1
                for c in range(3):
                    K.mm(psa, WQ[:, c, h * 192 + 128: h * 192 + 192], CQN[:, c, cols], start=(c == 0), stop=(c == 2))
                for c in range(3):
                    K.mm(psb, WQS[:, c, h, :], CQN[:, c, cols], start=(c == 0), stop=(c == 2))
                K.tt(T1, psa, COS[:, cols], ALU.mult)
                K.tt(T2, psb, SIN[:, cols], ALU.mult)
                K.tt(QR[0:64, cols], T1, T2, ALU.add)
                ps = K.ps(pj_banks[pj % 4]); pj += 1
                for c in range(2):
                    K.mm(ps, WKV[:, c, h * 256: h * 256 + 128], CKVN[:, c, cols], start=(c == 0), stop=(c == 1))
                K.copy(KN[:, cols], ps, eng="act")
            for kt in range(S // P):
                if kt % 4 == 0:
                    vb = pj_banks[pj % 4]; pj += 1
                ps = K.ps(vb, P, col0=(kt % 4) * P)
                for c in range(2):
                    K.mm(ps, CKVN[:, c, kt * P:(kt + 1) * P], WKV[:, c, h * 256 + 128: h * 256 + 256],
                         start=(c == 0), stop=(c == 1))
                if kt % 4 == 3:
                    K.copy(VH[:, kt - 3:kt + 1, :], K.ps(vb).rearrange("p (a b) -> p a b", a=4),
                           eng=("act" if (kt // 4) % 2 else "dve"))
            nkc = S // P
            nit = (S // W) * nkc

            def s_mm(i):
                qb_, kc_ = divmod(i, nkc)
                qc_ = slice(qb_ * W, (qb_ + 1) * W)
                kcs_ = slice(kc_ * P, (kc_ + 1) * P)
                pss_ = K.ps((0, 1, 2)[i % 3])
                K.mm(pss_, KN[:, kcs_], QN[:, qc_], start=True, stop=False)
                K.mm(pss_, KR[:, kcs_], QR[:, qc_], start=False, stop=True)

            s_mm(0)
            s_mm(1)
            for i in range(nit):
                qb, kc = divmod(i, nkc)
                qc = slice(qb * W, (qb + 1) * W)
                pso = K.ps(3 + (qb % 2))
                psz = K.ps(5 + (qb % 2))
                pss = K.ps((0, 1, 2)[i % 3])
                pt = PT[i % 3]
                K.act(pt, pss, AF.Exp, scale=scale)
                if i + 2 < nit:
                    s_mm(i + 2)
                K.mm(pso, VH[:, kc, :], pt, start=(kc == 0), stop=(kc == nkc - 1))
                K.mm(psz, self.ONESB, pt, start=(kc == 0), stop=(kc == nkc - 1))
                if kc == nkc - 1:
                    K.recip(RZ, psz)
                    K.tt(OALL[:, h, qc], pso, RZ, ALU.mult)
        K.release(m0)
        self.set_banks([0, 1, 2, 7])
        self.epilogue_full(OALL, self.mla_w_out, G_OFF + 1 * 8)

    def ffn(self, s, l):
        K = self.K
        XT = self.XT
        w_in_v = self.ffn_w_in[l].rearrange("(c p) n -> p c n", p=P)
        w_out_v = self.ffn_w_out[l].rearrange("(k p) n -> p k n", p=P)
        cw = lambda tap, j: self.g(CW_OFF + (l * 3 + tap) * NFF + j)
        cb = lambda j: self.g(CB_OFF + l * NFF + j)
        HL = S // 2
        mh = K.mark()
        HSAVE = K.sb([P, KC, 1], BF16)
        for hf in range(2):
            m0 = K.mark()
            t0 = hf * HL
            lo = t0
            hi = min(t0 + HL + 1, S)
            HNH, e1 = K.sb_at(self.BIGA_OFF, [P, KC, HL + 2], BF16)
            WOS0, e2 = K.sb_at(e1, [P, NFF, P], BF16)
            WOS1, e3 = K.sb_at(e2, [P, NFF, P], BF16)
            assert e3 <= self.BIGA_OFF + KC * S * 2
            ACTH = K.sb([P, NFF, HL], BF16)
            m1 = K.mark()
            tm = self.norm_tmps()
            self.set_banks([4, 5])
            self.prologue(lo, hi, G_OFF + (l * 4 + 2) * 8, HNH[:, :, lo - (t0 - 1):hi - (t0 - 1)], tm)
            if hf == 1:
                K.copy(HNH[:, :, 0:1], HSAVE)
            else:
                K.copy(HSAVE, HNH[:, :, HL:HL + 1])
            K.release(m1)
            WI = [K.sb([P, KC, 2, P], BF16) for _ in range(3)]
            Y = [K.sb([P, W], F32) for _ in range(2)]
            A = [K.sb([P, W], BF16) for _ in range(2)]

            def load(j):
                b = WI[j % 3]
                K.dma(b[:, :, 0, :], w_in_v[:, :, j * P:(j + 1) * P], eng="pool")
                K.dma(b[:, :, 1, :], w_in_v[:, :, DFF + j * P:DFF + (j + 1) * P], eng="pool")

            load(0)
            load(1)
            it = 0
            for j in range(NFF):
                if j + 2 < NFF:
                    load(j + 2)
                wi = WI[j % 3]
                for q in range(HL // W):
                    tq = t0 + q * W
                    ci = 1 + q * W
                    psg = K.ps(0 + (it % 2))
                    psv = K.ps(2 + (it % 2))
                    psh = K.ps(6 + (it % 2), 2)
                    y = Y[it % 2]
                    a = A[it % 2]
                    it += 1
                    for kc in range(KC):
                        K.mm(psg, wi[:, kc, 0, :], HNH[:, kc, ci:ci + W], start=(kc == 0), stop=(kc == KC - 1))
                    for kc in range(KC):
                        K.mm(psv, wi[:, kc, 1, :], HNH[:, kc, ci:ci + W], start=(kc == 0), stop=(kc == KC - 1))
                    has_l = tq > 0
                    has_r = tq + W < S
                    if has_l and has_r:
                        for kc in range(KC):
                            K.mm(psh[:, 0:2], wi[:, kc, 0, :], HNH[:, kc, ci - 1:ci + W + 1:W + 1],
                                 start=(kc == 0), stop=(kc == KC - 1))
                    elif has_l:
                        for kc in range(KC):
                            K.mm(psh[:, 0:1], wi[:, kc, 0, :], HNH[:, kc, ci - 1:ci],
                                 start=(kc == 0), stop=(kc == KC - 1))
                    elif has_r:
                        for kc in range(KC):
                            K.mm(psh[:, 1:2], wi[:, kc, 0, :], HNH[:, kc, ci + W:ci + W + 1],
                                 start=(kc == 0), stop=(kc == KC - 1))
                    K.act(y, psg, AF.Identity, scale=cw(1, j), bias=cb(j))
                    K.stt(y[:, 1:W], psg[:, 0:W - 1], cw(0, j), y[:, 1:W], ALU.mult, ALU.add)
                    K.stt(y[:, 0:W - 1], psg[:, 1:W], cw(2, j), y[:, 0:W - 1], ALU.mult, ALU.add)
                    if has_l:
                        K.stt(y[:, 0:1], psh[:, 0:1], cw(0, j), y[:, 0:1], ALU.mult, ALU.add)
                    if has_r:
                        K.stt(y[:, W - 1:W], psh[:, 1:2], cw(2, j), y[:, W - 1:W], ALU.mult, ALU.add)
                    K.act(a, y, AF.Gelu_apprx_tanh)
                    K.tt(ACTH[:, j, q * W:(q + 1) * W], psv, a, ALU.mult)
            K.release(m1)
            WOS = [WOS0, WOS1]
            M2 = K.sb([P, HL // W, KC, W], F32)
            tm = self.norm_tmps()
            Tb = [K.sb([P, W], F32) for _ in range(2)]
            K.dma(WOS[0], w_out_v[:, :, 0:P], eng="pool")
            for dc in range(KC):
                if dc + 1 < KC:
                    K.dma(WOS[(dc + 1) % 2], w_out_v[:, :, (dc + 1) * P:(dc + 2) * P], eng="pool")
                wo = WOS[dc % 2]
                for bq in range(HL // W):
                    ps = K.ps(4 + ((dc * 2 + bq) % 2))
                    for k in range(NFF):
                        K.mm(ps, wo[:, k, :], ACTH[:, k, bq * W:(bq + 1) * W], start=(k == 0), stop=(k == NFF - 1))
                    K.copy(M2[:, bq, dc, :], ps, eng="act")
            self.set_banks([6, 7])
            for bq in range(HL // W):
                self.post_norm_residual(M2[:, bq], G_OFF + (l * 4 + 3) * 8, t0 + bq * W, tm, Tb)
            K.release(m0)
        K.release(mh)

    def hgrn(self, s):
        K = self.K
        XT = self.XT
        HN = self.BIGA
        NT = S // P
        m0 = K.mark()
        OB = K.sb([P, NH, S], BF16)
        m1 = K.mark()
        tm = self.norm_tmps()
        self.set_banks(range(8))
        self.prologue(0, S, G_OFF + (1 * 4 + 0) * 8, HN, tm)
        K.release(m1)
        w_in_v = self.hgrn_w_in.rearrange("(c p) n -> p c n", p=P)
        WH = K.sb([P, 5, KC, P], BF16)
        VHm = [K.sb([P, NT, P], BF16) for _ in range(2)]
        K.memset(VHm[0][64:128, :, :], 0.0)
        K.memset(VHm[1][0:64, :, :], 0.0)
        QT = [K.sb([P, S], BF16) for _ in range(2)]
        KTl = [K.sb([P, S], BF16) for _ in range(2)]
        REF = [K.sb([P, 32], F32) for _ in range(2)]
        LAM = [K.sb([P, 32], F32) for _ in range(2)]
        DLT = K.sb([P, 32], F32)
        qs_off = (K.top + CELL - 1) // CELL * CELL
        QS = K.sb([P, S], F32)
        mh = K.mark()
        qscale = float(128 ** -0.5)
        one_b = self.ONE[:, 0:1].to_broadcast([P, S])

        def load_wh(h, parts):
            for i in parts:
                K.dma(WH[:, i, :, :], w_in_v[:, :, i * D + h * P: i * D + (h + 1) * P], eng="pool")

        load_wh(0, range(5))
        for h in range(NH):
            K.release(mh)
            for tb in range(S // W):
                cols = slice(tb * W, (tb + 1) * W)
                ps = K.ps(6 + (tb % 2))
                for kc in range(KC):
                    K.mm(ps, WH[:, 0, kc, :], HN[:, kc, cols], start=(kc == 0), stop=(kc == KC - 1))
                K.act(QS[:, cols], ps, AF.Silu)
            for dr in range(2):
                lcol = dr * 8 + h
                K.release(mh)
                PP = K.sb([P, S], F32)
                KF = K.sb([P, S], F32)
                EE = K.sb([P, S // 2], F32)
                if dr == 0:
                    for tb in range(S // W):
                        cols = slice(tb * W, (tb + 1) * W)
                        ps = K.ps(6 + (tb % 2))
                        for kc in range(KC):
                            K.mm(ps, WH[:, 1, kc, :], HN[:, kc, cols], start=(kc == 0), stop=(kc == KC - 1))
                        K.act(PP[:, cols], ps, AF.Sigmoid)
                    for kt in range(NT):
                        ps = K.ps(kt // 4, P, col0=(kt % 4) * P)
                        for kc in range(KC):
                            K.mm(ps, HN[:, kc, kt * P:(kt + 1) * P], WH[:, 3, kc, :],
                                 start=(kc == 0), stop=(kc == KC - 1))
                    psbw = []
                    for tb in range(S // W):
                        cols = slice(tb * W, (tb + 1) * W)
                        ps = K.ps(4 + tb)
                        for kc in range(KC):
                            K.mm(ps, WH[:, 2, kc, :], HN[:, kc, cols], start=(kc == 0), stop=(kc == KC - 1))
                        psbw.append(ps)
                else:
                    for g_ in range(NT // 4):
                        pv = K.ps(g_).rearrange("p (a b) -> p a b", a=4)
                        K.copy(VHm[0][0:64, 4 * g_:4 * g_ + 4, :], pv[0:64], eng="dve")
                        K.copy(VHm[1][64:128, 4 * g_:4 * g_ + 4, :], pv[64:128], eng="dve")
                    for tb in range(S // W):
                        cols = slice(tb * W, (tb + 1) * W)
                        K.act(PP[:, cols], psbw[tb], AF.Sigmoid)
                K.ts(KF, PP, self.NOML[:, lcol:lcol + 1], self.OML[:, lcol:lcol + 1], ALU.mult, ALU.add)
                K.act(PP, PP, AF.Ln, scale=self.OML[:, lcol:lcol + 1], bias=self.LB[:, lcol:lcol + 1])
                K.scan(PP, one_b, PP, 0.0, ALU.mult, ALU.add)
                if dr == 1:
                    for hf_ in range(2):
                        hc_ = slice(hf_ * (S // 2), (hf_ + 1) * (S // 2))
                        K.act(EE, KF[:, hc_], AF.Ln, scale=-1.0, bias=self.ONE[:, 0:1])
                        K.tt(PP[:, hc_], EE, PP[:, hc_], ALU.subtract)
                PPv = PP.rearrange("p (n c) -> p n c", c=64)
                K.copy(REF[dr], PPv[:, :, 31 + dr])
                if dr == 0:
                    K.tt(DLT[:, 0:31], REF[dr][:, 1:32], REF[dr][:, 0:31], ALU.subtract)
                    K.act(LAM[dr][:, 0:31], DLT[:, 0:31], AF.Exp)
                else:
                    K.tt(DLT[:, 1:32], REF[dr][:, 0:31], REF[dr][:, 1:32], ALU.subtract)
                    K.act(LAM[dr][:, 1:32], DLT[:, 1:32], AF.Exp)
                K.tt(PPv, PPv, REF[dr].unsqueeze(2).to_broadcast([P, 32, 64]), ALU.subtract)
                for hf_ in range(2):
                    hc_ = slice(hf_ * (S // 2), (hf_ + 1) * (S // 2))
                    K.act(EE, PP[:, hc_], AF.Exp)
                    K.stt(QT[dr][:, hc_], QS[:, hc_], qscale, EE, ALU.mult, ALU.mult)
                for hf_ in range(2):
                    hc_ = slice(hf_ * (S // 2), (hf_ + 1) * (S // 2))
                    K.act(EE, PP[:, hc_], AF.Exp, scale=-1.0)
                    K.tt(KTl[dr][:, hc_], KF[:, hc_], EE, ALU.mult)
            K.release(mh)
            if h + 1 < NH:
                load_wh(h + 1, range(4))
            KTT = []
            off = qs_off
            for dr in range(2):
                v, off = K.sb_at(off, [P, NT, P], BF16)
                KTT.append(v)
            for dr in range(2):
                for kt in range(NT):
                    if kt % 8 == 0:
                        tbk = 6 + ((kt // 8) % 2)
                    K.transpose(K.ps(tbk, 64, dt=BF16, col0=(kt % 8) * 64), KTl[dr][:, kt * P:(kt + 1) * P], self.IDB)
                    if kt % 8 == 7:
                        K.copy(KTT[dr][:, kt - 7:kt + 1, :], K.ps(tbk, dt=BF16).rearrange("p (a b) -> p a b", a=8),
                               eng=("dve" if dr == 0 else "act"))
            OS = K.sb([P, S], F32)
            mo = K.mark()
            Tst = [[K.sb([P, P], F32) for _ in range(4)] for _ in range(2)]
            TBs = [[K.sb([P, P], BF16) for _ in range(8)] for _ in range(2)]
            order = [list(range(NT)), list(range(NT - 1, -1, -1))]
            chunk_seq = [[], []]
            for dr in range(2):
                for p_ in order[dr]:
                    chunk_seq[dr] += ([2 * p_, 2 * p_ + 1] if dr == 0 else [2 * p_ + 1, 2 * p_])
            tb_of = [{}, {}]
            st = [dict(ti=0, bi=0, prevT=None, cnt=0) for _ in range(2)]
            PSA = (0, 1)
            PSD = ((2, 3), (4, 5))
            PSO = (6, 7)

            def produce(dr, pi):
                p_ = order[dr][pi]
                cols = slice(p_ * P, (p_ + 1) * P)
                psa = K.ps(PSA[dr], P)
                K.mm(psa, KTl[dr][:, cols], QT[dr][:, cols])
                am = self.AM[dr * 2 + (pi % 2)]
                K.stt(am, psa, 3.0e38, self.MASK[dr], ALU.min, ALU.mult)
                pair = chunk_seq[dr][2 * pi: 2 * pi + 2]
                psds = []
                for ci_, n in enumerate(pair):
                    psd = K.ps(PSD[dr][pi % 2], P, col0=ci_ * P)
                    K.mm(psd, KTT[dr][:, p_, :], VHm[n % 2][:, p_, :])
                    psds.append(psd)
                sd = st[dr]
                for ci_, n in enumerate(pair):
                    psd = psds[ci_]
                    Tn = Tst[dr][sd["ti"] % 4]
                    sd["ti"] += 1
                    if sd["prevT"] is None:
                        K.copy(Tn, psd, eng="dve")
                    else:
                        lam_in = (n - 1) if dr == 0 else (n + 1)
                        K.stt(Tn, sd["prevT"], LAM[dr][:, lam_in:lam_in + 1], psd, ALU.mult, ALU.add)
                    sd["prevT"] = Tn
                    sd["cnt"] += 1
                    if sd["cnt"] < 2 * NT:
                        tbn = TBs[dr][sd["bi"] % 8]
                        sd["bi"] += 1
                        K.act(tbn, Tn, AF.Copy, scale=LAM[dr][:, n:n + 1])
                        tb_of[dr][n] = tbn

            def consume(dr, pi):
                p_ = order[dr][pi]
                cols = slice(p_ * P, (p_ + 1) * P)
                pso = K.ps(PSO[dr], P)
                am = self.AM[dr * 2 + (pi % 2)]
                mms = [(pso, VHm[0][:, p_, :], am), (pso, VHm[1][:, p_, :], am)]
                for n in chunk_seq[dr][2 * pi: 2 * pi + 2]:
                    hh = n % 2
                    pred = (n - 1) if dr == 0 else (n + 1)
                    if pred in tb_of[dr]:
                        mms.append((pso[:, 64 * hh:64 * hh + 64], tb_of[dr][pred],
                                    QT[dr][:, p_ * P + 64 * hh: p_ * P + 64 * hh + 64]))
                for i_, (o_, l_, r_) in enumerate(mms):
                    K.mm(o_, l_, r_, start=(i_ == 0), stop=(i_ == len(mms) - 1))
                first = (dr == 0 and p_ < NT // 2) or (dr == 1 and p_ >= NT // 2)
                if first:
                    K.copy(OS[:, cols], pso, eng="act")
                else:
                    K.tt(OS[:, cols], pso, OS[:, cols], ALU.add)

            for dr in range(2):
                produce(dr, 0)
            for pi in range(NT):
                for dr in range(2):
                    if pi + 1 < NT:
                        produce(dr, pi + 1)
                    consume(dr, pi)
            K.release(mo)
            SG = K.sb([P, S], F32)
            SQ1 = K.sb([P, 2, W], BF16)
            SD = K.sb([P, 2 * W], F32)
            for tb in range(S // W):
                cols = slice(tb * W, (tb + 1) * W)
                ps = K.ps(2 + (tb % 2))
                for kc in range(KC):
                    K.mm(ps, WH[:, 4, kc, :], HN[:, kc, cols], start=(kc == 0), stop=(kc == KC - 1))
                K.act(SG[:, cols], ps, AF.Silu)
            for hf in range(2):
                hc = slice(hf * 2 * W, (hf + 1) * 2 * W)
                K.act(SQ1, OS[:, hc].rearrange("p (a b) -> p a b", a=2), AF.Square)
                pst = K.psum[:, 4 * 512: 6 * 512]
                for a_ in range(2):
                    K.mm(K.ps(4 + a_), self.ONESB, SQ1[:, a_, :])
                K.act(SD, pst, AF.Ln, scale=1.0 / P, bias=self.EPS[:, 0:1])
                K.act(SD, SD, AF.Exp, scale=-0.5)
                K.stt(OS[:, hc], OS[:, hc], self.g(ONG_OFF + h), SD, ALU.mult, ALU.mult)
                K.tt(OB[:, h, hc], OS[:, hc], SG[:, hc], ALU.mult)
            if h + 1 < NH:
                load_wh(h + 1, [4])
        K.release(m0)
        OBk = K.sb([P, NH, S], BF16)
        self.set_banks([0, 1, 2, 7])
        self.epilogue_full(OBk, self.hgrn_w_out, G_OFF + (1 * 4 + 1) * 8)
        K.release(m0)


def _cols(v):
    v = np.asarray(v, dtype=np.float32)
    return np.ascontiguousarray(v.reshape(-1, P).T)


def make_consts(pre_mix_norm, post_mix_norm, pre_ffn_norm, post_ffn_norm, mla_q_norm, mla_kv_norm,
                hgrn_out_norm, hgrn_lb_logits, ffn_conv_w, ffn_conv_b):
    c = np.zeros((P, NCST), np.float32)
    for l in range(2):
        for kind, arr in enumerate((pre_mix_norm, post_mix_norm, pre_ffn_norm, post_ffn_norm)):
            o = G_OFF + (l * 4 + kind) * 8
            c[:, o:o + 8] = _cols(arr[l])
    c[:, QN_OFF:QN_OFF + 3] = _cols(mla_q_norm[0])
    c[:, KVN_OFF:KVN_OFF + 2] = _cols(mla_kv_norm[0])
    c[:, ONG_OFF:ONG_OFF + 8] = _cols(hgrn_out_norm[0])
    for d in range(2):
        for l in range(2):
            o = LBL_OFF + (d * 2 + l) * 8
            c[:, o:o + 8] = _cols(hgrn_lb_logits[d, l])
    for l in range(2):
        for tap in range(3):
            o = CW_OFF + (l * 3 + tap) * NFF
            c[:, o:o + NFF] = _cols(ffn_conv_w[l, tap])
        o = CB_OFF + l * NFF
        c[:, o:o + NFF] = _cols(ffn_conv_b[l])
    inv = (1.0 / (np.float32(10000.0) ** (np.arange(0, 64, 2, dtype=np.float32) / np.float32(64)))).astype(np.float32)
    c[0:32, INVF_OFF] = inv
    c[32:64, INVF_OFF] = inv
    return c


_PROG_CACHE = {}


def get_prog(nseq=NSEQ, stages=("mla", "ffn0", "hgrn", "ffn1")):
    key = (nseq, tuple(stages))
    if key not in _PROG_CACHE:
        _PROG_CACHE[key] = Prog(nseq, stages)
    return _PROG_CACHE[key]


def core_inputs(x_c, pos_c, cst, w):
    m = {"x": np.ascontiguousarray(x_c, dtype=np.float32),
         "pos": np.ascontiguousarray(pos_c, dtype=np.int32),
         "cst": cst}
    m.update(w)
    return m


def weight_map(mla_w_in, mla_w_q_up, mla_w_kv_up, mla_w_out, hgrn_w_in, hgrn_w_out, ffn_w_in, ffn_w_out):
    f = lambda a: np.ascontiguousarray(np.asarray(a, dtype=np.float32))
    return {"mla_w_in": f(mla_w_in[0]), "mla_w_q_up": f(mla_w_q_up[0]), "mla_w_kv_up": f(mla_w_kv_up[0]),
            "mla_w_out": f(mla_w_out[0]), "hgrn_w_in": f(hgrn_w_in[0]), "hgrn_w_out": f(hgrn_w_out[0]),
            "ffn_w_in": f(ffn_w_in), "ffn_w_out": f(ffn_w_out)}


def kernel(x, positions, pre_mix_norm, post_mix_norm, pre_ffn_norm, post_ffn_norm,
           mla_w_in, mla_q_norm, mla_w_q_up, mla_kv_norm, mla_w_kv_up, mla_w_out,
           hgrn_w_in, hgrn_lb_logits, hgrn_out_norm, hgrn_w_out,
           ffn_w_in, ffn_conv_w, ffn_conv_b, ffn_w_out):
    n_cores = 8
    x = np.asarray(x)
    positions = np.asarray(positions)
    cst = make_consts(np.asarray(pre_mix_norm), np.asarray(post_mix_norm), np.asarray(pre_ffn_norm),
                      np.asarray(post_ffn_norm), np.asarray(mla_q_norm), np.asarray(mla_kv_norm),
                      np.asarray(hgrn_out_norm), np.asarray(hgrn_lb_logits), np.asarray(ffn_conv_w),
                      np.asarray(ffn_conv_b))
    w = weight_map(mla_w_in, mla_w_q_up, mla_w_kv_up, mla_w_out, hgrn_w_in, hgrn_w_out, ffn_w_in, ffn_w_out)
    prog = get_prog()
    in_maps = [core_inputs(x[c * NSEQ:(c + 1) * NSEQ], positions[c * NSEQ:(c + 1) * NSEQ], cst, w)
               for c in range(n_cores)]
    res = run_bass_kernel_spmd(prog.nc, in_maps, core_ids=list(range(n_cores)))
    out = np.concatenate([np.asarray(r["out"]) for r in res.results], axis=0)
    return out.astype(np.float32, copy=False)
```
